# Optimizing a Trainium2 kernel written in Bass

```python
import math
import jax, jax.numpy as jnp
from jax import lax
import numpy as np

D_MODEL = 1024
BATCH = 16
SEQ = 2048
DEPTH = 2
DEC_BATCH = 32
DEC_SEQ = 8
PAST_LEN = 16384
PAGE_SIZE = 128

GLA_HEADS = 4
GLA_DK = D_MODEL // 2
GLA_DV = D_MODEL
GLA_DK_HEAD = GLA_DK // GLA_HEADS
GLA_DV_HEAD = GLA_DV // GLA_HEADS
GATE_RANK = 16
GATE_TAU = 16.0
GLA_CHUNK = 64
WINDOWS = (128, 512, 2048)
DILATIONS = (1, 4, 16)
N_GROUPS = 3
HEAD_DIM = 64
HEADS_PER_GROUP = D_MODEL // HEAD_DIM
KV_HEADS = 4
Q_PER_KV = HEADS_PER_GROUP // KV_HEADS
Q_BLOCK = 128
N_BUCKETS = 32
MAX_EXACT = 16
MAX_DISTANCE = 2048
D_FF = -(-8 * D_MODEL // (3 * 256)) * 256
EPS = 1e-6

kernel_name = 'yoco_gla_dilated_swa_decode_step'


def rmsnorm(x, g):
    x32 = x.astype(jnp.float32)
    y = x32 * lax.rsqrt(jnp.mean(x32 * x32, axis=-1, keepdims=True) + EPS)
    return (y * g.astype(jnp.float32)).astype(x.dtype)


def swiglu(u, w_gate_up, w_down):
    gate, up = jnp.split(u @ w_gate_up, 2, axis=-1)
    return (jax.nn.silu(gate) * up) @ w_down


def t5_buckets(dist):
    d = np.asarray(dist)
    large = MAX_EXACT + (np.log(np.maximum(d, 1) / MAX_EXACT) / np.log(MAX_DISTANCE / MAX_EXACT)
                         * (N_BUCKETS - MAX_EXACT)).astype(np.int64)
    large = np.minimum(large, N_BUCKETS - 1)
    return np.where(d < MAX_EXACT, d, large).astype(np.int32)


def gla_recurrence(q, k, v, g, s0):
    B, T, H, DK = q.shape
    DV = v.shape[-1]
    c = math.gcd(T, GLA_CHUNK)
    n = T // c

    def blocks(a):
        return a.astype(jnp.float32).reshape(B, n, c, H, a.shape[-1]).transpose(1, 0, 3, 2, 4)

    causal = jnp.tril(jnp.ones((c, c), dtype=bool))

    def step(s, inp):
        qc, kc, vc, gc = inp
        b = jnp.cumsum(gc, axis=2)
        o_inter = jnp.einsum('bhid,bhde->bhie', qc * jnp.exp(b), s)
        diff = b[:, :, :, None, :] - b[:, :, None, :, :]
        decay = jnp.exp(jnp.where(causal[:, :, None], diff, -jnp.inf))
        attn = jnp.sum(qc[:, :, :, None, :] * kc[:, :, None, :, :] * decay, axis=-1)
        o_intra = jnp.einsum('bhij,bhje->bhie', attn, vc)
        b_last = b[:, :, -1, :]
        s_new = jnp.exp(b_last)[..., None] * s + jnp.einsum(
            'bhjd,bhje->bhde', kc * jnp.exp(b_last[:, :, None, :] - b), vc)
        return s_new, o_inter + o_intra

    s, o = lax.scan(step, s0.astype(jnp.float32), (blocks(q), blocks(k), blocks(v), blocks(g)))
    o = o.transpose(1, 0, 3, 2, 4).reshape(B, T, H, DV)
    return o, s


def gla_mixer(u, s0, w_in, w_a2, b_a, g_onorm, w_o):
    B, T, _ = u.shape
    p = u @ w_in
    q, k, v, r, a = jnp.split(p, [GLA_DK, 2 * GLA_DK, 2 * GLA_DK + GLA_DV, 2 * GLA_DK + 2 * GLA_DV], axis=-1)
    q = q.reshape(B, T, GLA_HEADS, GLA_DK_HEAD) * GLA_DK_HEAD ** -0.5
    k = k.reshape(B, T, GLA_HEADS, GLA_DK_HEAD)
    v = v.reshape(B, T, GLA_HEADS, GLA_DV_HEAD)
    g = jax.nn.log_sigmoid((a @ w_a2 + b_a).astype(jnp.float32)) / GATE_TAU
    g = g.reshape(B, T, GLA_HEADS, GLA_DK_HEAD)
    o, s = gla_recurrence(q, k, v, g, s0)
    o = rmsnorm(o, g_onorm).astype(u.dtype).reshape(B, T, GLA_DV)
    o = o * jax.nn.silu(r)
    return o @ w_o, s.astype(u.dtype)


def dilated_group(q, kv, start, dil, n_keys, bias):
    Tq = q.shape[1]
    q_pos = start + jnp.arange(Tq)
    idx = q_pos[:, None] - dil * jnp.arange(n_keys)[None, :]
    valid = idx >= 0
    kvg = kv[:, jnp.maximum(idx, 0)]
    s = jnp.einsum('bqhgd,bqkhd->bqhgk', q, kvg[:, :, :, 0]).astype(jnp.float32) * HEAD_DIM ** -0.5
    s = jnp.where(valid[None, :, None, None, :], s + bias, -jnp.inf)
    m = jnp.max(s, axis=-1, keepdims=True)
    p = jnp.exp(s - m)
    l = jnp.sum(p, axis=-1, keepdims=True)
    o = jnp.einsum('bqhgk,bqkhd->bqhgd', (p / l).astype(kv.dtype), kvg[:, :, :, 1])
    lse = (m + jnp.log(l))[..., 0]
    return o, lse


def mix_groups(q, starts, kvs, biases):
    outs, lses = [], []
    for gi in range(N_GROUPS):
        o, lse = dilated_group(q[:, :, gi], kvs[gi], starts[gi], DILATIONS[gi],
                               WINDOWS[gi] // DILATIONS[gi] + 1, biases[gi])
        outs.append(o.astype(jnp.float32))
        lses.append(lse)
    w = jax.nn.softmax(jnp.stack(lses, axis=0), axis=0)
    o = jnp.sum(w[..., None] * jnp.stack(outs, axis=0), axis=0)
    return o.astype(q.dtype)


def dilated_mixer(u, kvs, starts, w_q, w_o, biases):
    B, T, _ = u.shape
    q = (u @ w_q).reshape(B, T, N_GROUPS, KV_HEADS, Q_PER_KV, HEAD_DIM)
    if starts is None:
        nblk = T // Q_BLOCK
        qb = q.reshape(B, nblk, Q_BLOCK, N_GROUPS, KV_HEADS, Q_PER_KV, HEAD_DIM).swapaxes(0, 1)

        def blk(args):
            qi, i = args
            start = i * Q_BLOCK
            return mix_groups(qi, (start,) * N_GROUPS, kvs, biases)

        o = lax.map(blk, (qb, jnp.arange(nblk)))
        o = o.swapaxes(0, 1).reshape(B, T, HEADS_PER_GROUP * HEAD_DIM)
    else:
        o = mix_groups(q, starts, kvs, biases).reshape(B, T, HEADS_PER_GROUP * HEAD_DIM)
    return o @ w_o


def trunk(x, gla_s0, win_past, norm_g, w_in_a, w_a2, b_a, g_onorm, w_o_a, g_kv, w_kv,
          w_q_b, w_o_b, rel_bias, w_gate_up, w_down):
    n_a = DEPTH // 2
    B, T, _ = x.shape
    biases = []
    for gi in range(N_GROUPS):
        nk = WINDOWS[gi] // DILATIONS[gi] + 1
        bk = jnp.asarray(t5_buckets(DILATIONS[gi] * np.arange(nk)))
        tab = rel_bias[bk][:, gi * HEADS_PER_GROUP:(gi + 1) * HEADS_PER_GROUP]
        biases.append(tab.T.reshape(KV_HEADS, Q_PER_KV, nk).astype(jnp.float32))
    h = x
    gla_states = []
    kvs, starts, win_new = None, None, None
    for l in range(DEPTH):
        if l < n_a:
            y, s = gla_mixer(rmsnorm(h, norm_g[l, 0]), gla_s0[l], w_in_a[l], w_a2[l], b_a[l],
                             g_onorm[l], w_o_a[l])
            gla_states.append(s)
        else:
            if l == n_a:
                kv = (rmsnorm(h, g_kv) @ w_kv).reshape(B, T, N_GROUPS, 2, KV_HEADS, HEAD_DIM)
                kvs, starts, win_new = [], [], []
                for gi in range(N_GROUPS):
                    new = kv[:, :, gi]
                    if win_past is None:
                        full, start = new, 0
                    else:
                        full = jnp.concatenate([win_past[gi].astype(new.dtype), new], axis=1)
                        start = win_past[gi].shape[1]
                    kvs.append(full)
                    starts.append(start)
                    win_new.append(full[:, full.shape[1] - min(WINDOWS[gi], full.shape[1]):])
            y = dilated_mixer(rmsnorm(h, norm_g[l, 0]), kvs, None if win_past is None else tuple(starts),
                              w_q_b[l - n_a], w_o_b[l - n_a], biases)
        h = h + rmsnorm(y, norm_g[l, 1])
        h = h + rmsnorm(swiglu(rmsnorm(h, norm_g[l, 2]), w_gate_up[l], w_down[l]), norm_g[l, 3])
    return h, jnp.stack(gla_states, axis=0), win_new


def setup_inputs(seed: int = 0) -> dict:
    key = jax.random.key(seed)
    ks = jax.random.split(key, 20)
    n_a = DEPTH // 2
    n_b = DEPTH - n_a
    f32 = jnp.float32
    nrm = lambda k, shp, sc: jax.random.normal(k, shp, f32) * sc
    x_prompt = nrm(ks[0], (BATCH, SEQ, D_MODEL), 1.0)
    x_sample = nrm(ks[1], (DEC_BATCH, DEC_SEQ, D_MODEL), 1.0)
    state_gla = nrm(ks[2], (n_a, DEC_BATCH, GLA_HEADS, GLA_DK_HEAD, GLA_DV_HEAD), 0.5)
    cache_win1 = nrm(ks[3], (DEC_BATCH, min(WINDOWS[0], PAST_LEN), 2, KV_HEADS, HEAD_DIM), 1.0)
    cache_win2 = nrm(ks[4], (DEC_BATCH, min(WINDOWS[1], PAST_LEN), 2, KV_HEADS, HEAD_DIM), 1.0)
    cache_win3 = nrm(ks[5], (DEC_BATCH, min(WINDOWS[2], PAST_LEN), 2, KV_HEADS, HEAD_DIM), 1.0)
    norm_g = 1.0 + nrm(ks[6], (DEPTH, 4, D_MODEL), 0.02)
    w_in_a = nrm(ks[7], (n_a, D_MODEL, 2 * GLA_DK + 2 * GLA_DV + GATE_RANK), D_MODEL ** -0.5)
    w_a2 = nrm(ks[8], (n_a, GATE_RANK, GLA_DK), GATE_RANK ** -0.5)
    b_a = nrm(ks[9], (n_a, GLA_DK), 0.1)
    g_onorm = 1.0 + nrm(ks[10], (n_a, GLA_DV_HEAD), 0.02)
    w_o_a = nrm(ks[11], (n_a, GLA_DV, D_MODEL), GLA_DV ** -0.5)
    g_kv = 1.0 + nrm(ks[12], (D_MODEL,), 0.02)
    w_kv = nrm(ks[13], (D_MODEL, N_GROUPS * 2 * KV_HEADS * HEAD_DIM), D_MODEL ** -0.5)
    w_q_b = nrm(ks[14], (n_b, D_MODEL, N_GROUPS * HEADS_PER_GROUP * HEAD_DIM), D_MODEL ** -0.5)
    w_o_b = nrm(ks[15], (n_b, HEADS_PER_GROUP * HEAD_DIM, D_MODEL), (HEADS_PER_GROUP * HEAD_DIM) ** -0.5)
    rel_bias = nrm(ks[16], (N_BUCKETS, N_GROUPS * HEADS_PER_GROUP), 0.1)
    w_gate_up = nrm(ks[17], (DEPTH, D_MODEL, 2 * D_FF), D_MODEL ** -0.5)
    w_down = nrm(ks[18], (DEPTH, D_FF, D_MODEL), D_FF ** -0.5)
    return {'x_prompt': x_prompt, 'x_sample': x_sample, 'state_gla': state_gla,
            'cache_win1': cache_win1, 'cache_win2': cache_win2, 'cache_win3': cache_win3,
            'norm_g': norm_g, 'w_in_a': w_in_a, 'w_a2': w_a2, 'b_a': b_a, 'g_onorm': g_onorm,
            'w_o_a': w_o_a, 'g_kv': g_kv, 'w_kv': w_kv, 'w_q_b': w_q_b, 'w_o_b': w_o_b,
            'rel_bias': rel_bias, 'w_gate_up': w_gate_up, 'w_down': w_down}


def reference(x_prompt, x_sample, state_gla, cache_win1, cache_win2, cache_win3, norm_g, w_in_a,
              w_a2, b_a, g_onorm, w_o_a, g_kv, w_kv, w_q_b, w_o_b, rel_bias, w_gate_up, w_down):
    n_a = DEPTH // 2
    s0_prompt = jnp.zeros((n_a, x_prompt.shape[0], GLA_HEADS, GLA_DK_HEAD, GLA_DV_HEAD), x_prompt.dtype)
    y_prompt, gla_prompt, win_prompt = trunk(
        x_prompt, s0_prompt, None, norm_g, w_in_a, w_a2, b_a, g_onorm, w_o_a, g_kv, w_kv,
        w_q_b, w_o_b, rel_bias, w_gate_up, w_down)
    y_sample, gla_sample, win_sample = trunk(
        x_sample, state_gla, (cache_win1, cache_win2, cache_win3), norm_g, w_in_a, w_a2, b_a,
        g_onorm, w_o_a, g_kv, w_kv, w_q_b, w_o_b, rel_bias, w_gate_up, w_down)
    return (y_prompt, y_sample, gla_prompt, win_prompt[0], win_prompt[1], win_prompt[2],
            gla_sample, win_sample[0], win_sample[1], win_sample[2])
```

```python
import math
import numpy as np
from contextlib import ExitStack
import concourse.bass as bass
import concourse.mybir as mybir
from concourse.bass_utils import run_bass_kernel_spmd

F32 = mybir.dt.float32
BF16 = mybir.dt.bfloat16
AF = mybir.ActivationFunctionType
ALU = mybir.AluOpType

NCORES = 8
D = 1024
NT = 33
NTOK = NT * 128
SEQ = 2048
DFF = 2816
EPS = 1e-6
DILS = (1, 4, 16)
WINS = (128, 512, 2048)
OSLOT = 1040
DEBUG = False
SCR_EXT = False
STRICT = True
PROFILE_LABELS = False
_PHASES = []


def oslot(h):
    return (h // 7) * 512 + (h % 7) * 65


ENGS = ("pe", "act", "dve", "pool", "sp")


class Buf:
    __slots__ = ("name", "last_writer", "readers", "sem", "semval", "strict")

    def __init__(self, name):
        self.name = name
        self.strict = name.startswith("junk")
        self.last_writer = None
        self.readers = []
        self.sem = None
        self.semval = 0


class Op:
    __slots__ = ("eng", "fn", "deps", "is_dma", "marked", "val", "sembuf", "label")

    def __init__(self, eng, fn, is_dma, sembuf):
        self.label = None
        self.eng = eng
        self.fn = fn
        self.deps = []
        self.is_dma = is_dma
        self.marked = False
        self.val = 0
        self.sembuf = sembuf


class Phase:
    semstack = None

    def __init__(self, nc, name):
        self.nc = nc
        self.name = name
        self.ops = []
        self.bufs = []

    def setup_cast(self, sb):
        self.stg = [sb(f"stg{i}", [128, 1024], F32) for i in range(3)]
        self.Bstg = self.bufs_n("stg", 3)
        self.ncast = 0

    def load_cast(self, dst_of, src, ncols, b, step=1024):
        c0 = 0
        while c0 < ncols:
            c1 = min(ncols, c0 + step)
            i = self.ncast % 3
            eng = ("pool", "act", "dve")[self.ncast % 3]
            self.ncast += 1
            stg, Bs = self.stg[i], self.Bstg[i]
            self.load(stg[:, 0:c1 - c0], src[:, c0:c1], Bs)
            dst = dst_of(c0, c1)
            n = c1 - c0
            if eng == "act":
                self.op("act", lambda e, stg=stg, dst=dst, n=n: e.copy(out=dst, in_=stg[:, 0:n]), [Bs], [b])
            else:
                self.op(eng, lambda e, stg=stg, dst=dst, n=n: e.tensor_copy(out=dst, in_=stg[:, 0:n]), [Bs], [b])
            c0 = c1

    def buf(self, name):
        b = Buf(name)
        self.bufs.append(b)
        return b

    def bufs_n(self, name, n):
        return [self.buf(f"{name}{i}") for i in range(n)]

    def _record(self, op, reads, writes):
        deps = {}
        for b in reads:
            w = b.last_writer
            if w is not None:
                deps[id(w)] = (w, True)
        for b in writes:
            w = b.last_writer
            if w is not None and id(w) not in deps:
                deps[id(w)] = (w, b.strict)
            for r in b.readers:
                if id(r) not in deps:
                    deps[id(r)] = (r, False)
        for d, raw in deps.values():
            if d is op:
                continue
            if (not d.is_dma) and (not op.is_dma) and d.eng == op.eng:
                if STRICT:
                    if op.eng == "pe":
                        continue
                elif op.eng == "pool":
                    pass
                elif not (raw and op.eng in ("act", "dve")):
                    continue
            op.deps.append(d)
        for b in reads:
            b.readers.append(op)
        for b in writes:
            b.last_writer = op
            b.readers = []
        if PROFILE_LABELS:
            import sys as _sys
            f = _sys._getframe(1)
            while f is not None and f.f_code.co_name in ("_record", "op", "dma", "load", "store", "mm", "tr", "load_cast", "<lambda>", "proj", "rstd_ops"):
                f = f.f_back
            op.label = f.f_lineno if f is not None else None
        self.ops.append(op)

    def op(self, eng, fn, reads=(), writes=()):
        o = Op(eng, fn, False, None)
        self._record(o, reads, writes)
        return o

    def dma(self, fn, sembuf, reads=(), writes=(), eng="sp"):
        o = Op(eng, fn, True, sembuf)
        self._record(o, reads, writes)
        return o

    def load(self, out, in_, b, eng="sp", extra_reads=()):
        return self.dma(lambda e: e.dma_start(out=out, in_=in_), b, reads=extra_reads, writes=[b], eng=eng)

    def store(self, out, in_, b, extra_writes=()):
        return self.dma(lambda e: e.dma_start(out=out, in_=in_), b, reads=[b], writes=extra_writes)

    def mm(self, out, lhsT, rhs, start, stop, reads, writes):
        return self.op("pe", lambda e: e.matmul(out, lhsT, rhs, start=start, stop=stop), reads, writes)

    def tr(self, out, in_, ident, reads, writes):
        return self.op("pe", lambda e: e.transpose(out=out, in_=in_, identity=ident), reads, writes)

    def emit(self):
        nc = self.nc
        _PHASES.append(self)
        for o in self.ops:
            for d in o.deps:
                if not d.is_dma:
                    d.marked = True
        cnt = {e: 0 for e in ENGS}
        for o in self.ops:
            if o.is_dma:
                o.sembuf.semval += 16
                o.val = o.sembuf.semval
            elif o.marked:
                cnt[o.eng] += 1
                o.val = cnt[o.eng]
        with ExitStack() as st:
            gst = st
            esem = {e: gst.enter_context(nc.semaphore(f"{self.name}_s_{e}")) for e in ENGS if e != "sp"}
            allsems = list(esem.values())
            for b in self.bufs:
                if b.semval > 0:
                    b.sem = gst.enter_context(nc.semaphore(f"{self.name}_d_{b.name}"))
                    allsems.append(b.sem)
            with nc.Block() as cblock:
                @cblock.gpsimd
                def _(e):
                    for sm in allsems:
                        e.sem_clear(sm)
            block = st.enter_context(nc.Block())
            per = {e: [o for o in self.ops if o.eng == e] for e in ENGS}

            def replay(eng_name, eng):
                seen = {}
                for o in per[eng_name]:
                    need = {}
                    for d in o.deps:
                        if d.is_dma:
                            key = ("d", id(d.sembuf)); sem = d.sembuf.sem
                        else:
                            key = ("e", d.eng); sem = esem[d.eng]
                        if seen.get(key, 0) >= d.val:
                            continue
                        if key not in need or need[key][1] < d.val:
                            need[key] = (sem, d.val)
                    for key, (sem, val) in need.items():
                        eng.wait_ge(sem, val)
                        seen[key] = val
                    ins = o.fn(eng)
                    if o.is_dma:
                        ins.then_inc(o.sembuf.sem, 16)
                    elif o.marked:
                        ins.then_inc(esem[o.eng], 1)
                if eng_name == "sp":
                    for b in self.bufs:
                        if b.semval > 0 and seen.get(("d", id(b)), 0) < b.semval:
                            eng.wait_ge(b.sem, b.semval)

            @block.tensor
            def _(e):
                replay("pe", e)

            @block.scalar
            def _(e):
                replay("act", e)

            @block.vector
            def _(e):
                replay("dve", e)

            @block.gpsimd
            def _(e):
                replay("pool", e)

            @block.sync
            def _(e):
                replay("sp", e)


class Ctx:
    pass


def build():
    nc = bass.Bass("TRN2", target_bir_lowering=False)
    C = Ctx()
    C.nc = nc

    def din(name, shape, dt=F32):
        return nc.dram_tensor(name, list(shape), dt, kind="ExternalInput").ap()

    def dout(name, shape, dt=F32):
        return nc.dram_tensor(name, list(shape), dt, kind="ExternalOutput").ap()

    def dscr(name, shape, dt=F32):
        kind = "ExternalOutput" if (SCR_EXT or (DEBUG and name in ("H1", "H2", "H3"))) else "Internal"
        return nc.dram_tensor(name, list(shape), dt, kind=kind).ap()

    C.x = din("x", [NTOK, D])
    C.state = din("state", [4, 4, 128, 256])
    C.cache = [din(f"cache{g}", [4, WINS[g], 512]) for g in range(3)]
    C.gains = din("gains", [9, 128, D])
    C.gon = din("gon", [128, 256])
    C.w_in = din("w_in", [D, 3088])
    C.w_a2 = din("w_a2", [16, 512])
    C.b_a = din("b_a", [1, 512])
    C.w_o_a = din("w_o_a", [D, D])
    C.w_kv = din("w_kv", [D, 1536])
    C.w_q = din("w_q", [D, 3072])
    C.w_o_b = din("w_o_b", [D, D])
    C.w_gu = din("w_gu", [2, D, 2 * DFF])
    C.w_dn = din("w_dn", [2, DFF, D])
    C.btab = din("btab", [12, 128, 1024])
    C.consts = din("consts", [8, 128, 128])
    C.y = dout("y", [NTOK, D])
    C.gla_p = dout("gla_p", [2, 4, 128, 256])
    C.gla_s = dout("gla_s", [4, 4, 128, 256])
    C.win_p = [dout(f"win{g}_p", [2, WINS[g], 512]) for g in range(3)]
    C.win_s = [dout(f"win{g}_s", [4, WINS[g], 512]) for g in range(3)]
    C.H1 = dscr("H1", [NTOK, D])
    C.H2 = dscr("H2", [NTOK, D])
    C.H3 = dscr("H3", [NTOK, D])
    C.QT = dscr("QT", [24, 128, NTOK], BF16)
    C.KT = dscr("KT", [6, 128, NTOK], BF16)
    C.VA = dscr("VA", [3, NTOK, 264], BF16)
    C.OG = dscr("OG", [3, NTOK, OSLOT])
    if DEBUG:
        C.DBG = nc.dram_tensor("DBG", [NTOK, D], F32, kind="ExternalOutput").ap()

    C.semstack = ExitStack()
    Phase.semstack = C.semstack
    with C.semstack:
        phase_gla(C)
        phase_ffn(C, 0, C.H1, C.H2)
        phase_qkv(C)
        phase_attn(C)
        with nc.sbuf_tensor("wgu1_pre", [128, 8, 2 * DFF], BF16) as wgu1:
            phase_oproj(C, wgu1)
            phase_ffn(C, 1, C.H3, C.y, wgu1)
    return nc


def rstd_ops(P, Bnh, ss_ap, n, rs_ap, tmp_ap, neghalf_ap, Bss, Brs, Btmp, dim):
    P.op("dve", lambda e: e.tensor_scalar(out=tmp_ap, in0=ss_ap, scalar1=1.0 / dim, scalar2=EPS,
                                          op0=ALU.mult, op1=ALU.add), [Bss], [Btmp])
    P.op("pool", lambda e: e.tensor_tensor(out=rs_ap, in0=tmp_ap, in1=neghalf_ap, op=ALU.pow), [Btmp, Bnh], [Brs])


def phase_gla(C):
    nc = C.nc
    with ExitStack() as st:
        def sb(name, shape, dt):
            return st.enter_context(nc.sbuf_tensor("g_" + name, list(shape), dt))

        def ps(name, shape, dt):
            return st.enter_context(nc.psum_tensor("g_" + name, list(shape), dt))

        P = Phase(nc, "gla")
        P.setup_cast(sb)
        win = sb("win", [128, 8, 3088], BF16)
        wo = sb("wo", [128, 8, D], BF16)
        wa2 = sb("wa2", [16, 512], F32)
        ba = sb("ba", [1, 512], F32)
        ones1 = sb("ones1", [1, 128], F32)
        neghalf = sb("neghalf", [128, 8], F32)
        cst = sb("cst", [128, 8, 128], F32)
        identb = sb("identb", [128, 128], BF16)
        g0 = sb("g0", [128, D], F32)
        g1 = sb("g1", [128, D], F32)
        gon = sb("gon", [128, 256], F32)
        Bw, Bwo, Bsm, Bcst, Bidb, Bg = P.buf("win"), P.buf("wo"), P.buf("small"), P.buf("cst"), P.buf("identb"), P.buf("gains")
        Bwa2, Bba, Bgon, Bg1 = P.buf("wa2"), P.buf("ba"), P.buf("gon"), P.buf("g1")
        in_edges = [0, 1024, 2048, 3072, 3088]
        Bwin = P.bufs_n("win_blk", 4)
        wl_ = []
        for blk in (0, 1, 3, 2):
            c0, c1 = in_edges[blk], in_edges[blk + 1]
            for kc in range(8):
                wl_.append(lambda kc=kc, c0=c0, c1=c1, blk=blk: P.load_cast(
                    lambda a, b: win[:, kc, c0 + a:c0 + b], C.w_in[kc * 128:(kc + 1) * 128, c0:c1], c1 - c0, Bwin[blk]))
        for kc in range(8):
            wl_.append(lambda kc=kc: P.load_cast(lambda a, b: wo[:, kc, a:b], C.w_o_a[kc * 128:(kc + 1) * 128, :], D, Bwo))

        def pump(k):
            for _ in range(k):
                if wl_:
                    wl_.pop(0)()

        P.load(wa2[:], C.w_a2, Bwa2)
        P.load(ba[:], C.b_a, Bba)
        P.load(cst[:], C.consts.rearrange("c p n -> p c n"), Bcst)
        P.load(g0[:], C.gains[0], Bg)
        P.load(g1[:], C.gains[1], Bg1)
        P.load(gon[:], C.gon, Bgon)
        P.op("pool", lambda e: e.memset(ones1[:], 1.0), [], [Bsm])
        P.op("pool", lambda e: e.memset(neghalf[:], -0.5), [], [Bsm])
        P.op("dve", lambda e: e.tensor_copy(out=identb[:], in_=cst[:, 0, :]), [Bcst], [Bidb])

        xt = [sb(f"xt{i}", [128, D], F32) for i in range(3)]
        Bxt = P.bufs_n("xt", 3)
        junk = sb("junk", [128, D], BF16)
        Bjunk = P.buf("junk")
        ub = sb("ub", [128, D], BF16)
        Bub = P.buf("ub")
        uT = sb("uT", [128, 8, 128], BF16)
        BuT = P.buf("uT")
        qs2 = [sb(f"qs{i}", [128, 512], F32) for i in range(2)]; ks2 = [sb(f"ks{i}", [128, 512], F32) for i in range(2)]
        vb2 = [sb(f"vb{i}", [128, D], BF16) for i in range(2)]; sr2 = [sb(f"sr{i}", [128, D], BF16) for i in range(2)]
        spl2 = [sb(f"spl{i}", [128, 512], F32) for i in range(2)]
        Bqs2, Bks2, Bvb2, Bsr2, Bspl2 = (P.bufs_n(n, 2) for n in ("qs", "ks", "vb", "sr", "spl"))
        aT = sb("aT", [16, 128], F32)
        BaT = P.buf("aT")
        e1 = sb("e1", [128, 512], F32)
        eb = sb("eb", [128, 512], F32); enb = sb("enb", [128, 512], F32)
        Be1, Beb, Benb = P.buf("e1"), P.buf("eb"), P.buf("enb")
        dec = sb("dec", [128, 16], F32); Bdec = P.buf("dec")
        qt = sb("qt", [128, 512], BF16); kt = sb("kt", [128, 512], BF16)
        Bqt, Bkt = P.buf("qt"), P.buf("kt")
        ktm = sb("ktm", [128, 4, 512], BF16); Bktm = P.buf("ktm")
        qkT = sb("qkT", [128, 8, 128], BF16); BqkT = P.buf("qkT")
        qmT = sb("qmT", [128, 4, 4, 128], BF16); BqmT = P.buf("qmT")
        AT = sb("AT", [128, 4, 128], BF16); BAT = P.buf("AT")
        S = sb("S", [128, 16, 256], F32); Sb = sb("Sb", [128, 16, 256], BF16)
        BS, BSb = P.buf("S"), P.buf("Sb")
        Stmp = sb("Stmp", [128, 256], F32); BStmp = P.buf("Stmp")
        ss = sb("ss", [128, 8], F32); ms = sb("ms", [128, 8], F32); rs = sb("rs", [128, 8], F32)
        Bss, Bms, Brs = P.buf("ss"), P.buf("ms"), P.buf("rs")
        ss2 = sb("ss2", [128, 8], F32); ms2 = sb("ms2", [128, 8], F32); rs2 = sb("rs2", [128, 8], F32)
        Bss2, Bms2, Brs2 = P.buf("ss2"), P.buf("ms2"), P.buf("rs2")
        ss3 = sb("ss3", [128, 8], F32); ms3 = sb("ms3", [128, 8], F32); rs3 = sb("rs3", [128, 8], F32)
        Bss3, Bms3, Brs3 = P.buf("ss3"), P.buf("ms3"), P.buf("rs3")
        on = sb("on", [128, D], F32); Bon = P.buf("on")
        og = sb("og", [128, D], BF16); Bog = P.buf("og")
        ogT = sb("ogT", [128, 8, 128], BF16); BogT = P.buf("ogT")
        yt = sb("yt", [128, D], F32); Byt = P.buf("yt")
        h1 = [sb(f"h1_{i}", [128, D], F32) for i in range(2)]
        Bh1 = P.bufs_n("h1_", 2)
        pT = ps("pT", [128, 8, 128], BF16); BpT = P.buf("pT")
        pP = [ps(f"pP{i}", [128, 512], F32) for i in range(2)]; BpP = P.bufs_n("pP", 2)
        pS = ps("pS", [128, 512], F32); BpSa = P.buf("pSa"); BpSb = P.buf("pSb")
        pZ = ps("pZ", [128, 512], F32); BpZ = P.buf("pZ")
        pA = ps("pA", [128, 512], F32); BpA = P.buf("pA")
        pO = [ps(f"pO{i}", [128, 512], F32) for i in range(2)]; BpO = P.bufs_n("pO", 2)
        pZb = pZ[:].bitcast(BF16)
        pAb = pA[:].bitcast(BF16)

        P.op("pool", lambda e: e.memset(qmT[:], 0.0), [], [BqmT])

        def load_x(t):
            P.load(xt[t % 3][:], C.x[t * 128:(t + 1) * 128, :], Bxt[t % 3])

        pcount = [0]

        def proj_bank():
            i = pcount[0] % 2
            pcount[0] += 1
            return pP[i], BpP[i]

        def proj(n0, evac):
            bank, Bb = proj_bank()
            for kc in range(8):
                P.mm(bank[:], uT[:, kc, :], win[:, kc, n0:n0 + 512], kc == 0, kc == 7, [BuT, Bwin[n0 // 1024]], [Bb])
            evac(bank, Bb)

        def h1_chunks(t):
            X = xt[t % 3]; BX = Bxt[t % 3]
            i2 = t % 2
            qs, ks, vb, sr, spl = qs2[i2], ks2[i2], vb2[i2], sr2[i2], spl2[i2]
            Bqs, Bks, Bvb, Bsr, Bspl = Bqs2[i2], Bks2[i2], Bvb2[i2], Bsr2[i2], Bspl2[i2]

            def c1():
                P.op("act", lambda e: e.activation(out=junk[:], in_=X[:], func=AF.Square, accum_out=ss[:, 0:1]), [BX], [Bjunk, Bss])
                rstd_ops(P, Bsm, ss[:, 0:1], 1, rs[:, 0:1], ms[:, 0:1], neghalf[:, 0:1], Bss, Brs, Bms, D)
                P.op("dve", lambda e: e.scalar_tensor_tensor(out=ub[:], in0=X[:], scalar=rs[:, 0:1], in1=g0[:],
                                                             op0=ALU.mult, op1=ALU.mult), [BX, Brs, Bg], [Bub])
                for kc in range(8):
                    P.tr(pT[:, kc, :], ub[:, kc * 128:(kc + 1) * 128], identb[:], [Bub, Bidb], [BpT])
                P.op("act", lambda e: e.copy(out=uT[:], in_=pT[:]), [BpT], [BuT])

            def c2():
                proj(0, lambda bank, Bb: P.op("act", lambda e: e.copy(out=qs[:], in_=bank[:]), [Bb], [Bqs]))
                proj(512, lambda bank, Bb: P.op("dve", lambda e: e.tensor_copy(out=ks[:], in_=bank[:]), [Bb], [Bks]))

            def c3():
                proj(1024, lambda bank, Bb: P.op("act", lambda e: e.copy(out=vb[:, 0:512], in_=bank[:]), [Bb], [Bvb]))
                proj(1536, lambda bank, Bb: P.op("dve", lambda e: e.tensor_copy(out=vb[:, 512:1024], in_=bank[:]), [Bb], [Bvb]))
                for kc in range(8):
                    P.mm(pS[0:16, 0:128], win[:, kc, 3072:3088], uT[:, kc, :], kc == 0, kc == 7, [BuT, Bwin[3]], [BpSa])
                P.op("dve", lambda e: e.tensor_copy(out=aT[:], in_=pS[0:16, 0:128]), [BpSa], [BaT])

            def c4():
                bank, Bb = proj_bank()
                P.mm(bank[:], aT[:], wa2[:], True, False, [BaT, Bwa2], [Bb])
                P.mm(bank[:], ones1[:], ba[:], False, True, [Bsm, Bba], [Bb])
                P.op("act", lambda e: e.activation(out=e1[:], in_=bank[:], func=AF.Exp, scale=-1.0), [Bb], [Be1])
                P.op("act", lambda e: e.activation(out=spl[:], in_=e1[:], func=AF.Ln, bias=1.0), [Be1], [Bspl])
                proj(2048, lambda bank, Bb: P.op("act", lambda e: e.activation(out=sr[:, 0:512], in_=bank[:], func=AF.Silu), [Bb], [Bsr]))
                proj(2560, lambda bank, Bb: P.op("act", lambda e: e.activation(out=sr[:, 512:1024], in_=bank[:], func=AF.Silu), [Bb], [Bsr]))
            return [c1, c2, c3, c4]

        def h2_chunks(t):
            samp = t == 32
            nseg = 4 if samp else 1
            X = xt[t % 3]; BX = Bxt[t % 3]
            i2 = t % 2
            qs, ks, vb, sr, spl = qs2[i2], ks2[i2], vb2[i2], sr2[i2], spl2[i2]
            Bqs, Bks, Bvb, Bsr, Bspl = Bqs2[i2], Bks2[i2], Bvb2[i2], Bsr2[i2], Bspl2[i2]
            ucs = cst[:, 3, :] if samp else cst[:, 1, :]
            m01 = cst[:, 4, :] if samp else cst[:, 2, :]
            segneg = cst[:, 5, 1:5] if samp else cst[:, 5, 0:1]

            def init_state():
                if t == 0 or t == 16:
                    P.op("pool", lambda e: e.memset(S[:, 0:4, :], 0.0), [], [BS])
                    P.op("pool", lambda e: e.memset(Sb[:, 0:4, :], 0.0), [], [BSb])
                if samp:
                    P.load(S[:], C.state.rearrange("s h d e -> d (s h) e"), BS)
                    P.op("act", lambda e: e.copy(out=Sb[:], in_=S[:]), [BS], [BSb])

            def c1():
                bbank, Bbb = proj_bank()
                P.mm(bbank[:], ucs, spl[:], True, True, [Bcst, Bspl], [Bbb])
                for h in range(4):
                    P.mm(pS[:, 128 + h * nseg:128 + (h + 1) * nseg], spl[:, h * 128:(h + 1) * 128], segneg, True, True,
                         [Bspl, Bcst], [BpSb])
                P.op("act", lambda e: e.activation(out=eb[:], in_=bbank[:], func=AF.Exp), [Bbb], [Beb])
                P.op("act", lambda e: e.activation(out=enb[:], in_=bbank[:], func=AF.Exp, scale=-1.0), [Bbb], [Benb])
                P.op("act", lambda e: e.activation(out=dec[:, 0:4 * nseg], in_=pS[:, 128:128 + 4 * nseg], func=AF.Exp),
                     [BpSb], [Bdec])
                P.op("dve", lambda e: e.scalar_tensor_tensor(out=qt[:], in0=qs[:], scalar=128.0 ** -0.5, in1=eb[:],
                                                             op0=ALU.mult, op1=ALU.mult), [Bqs, Beb], [Bqt])
                P.op("dve", lambda e: e.tensor_tensor(out=kt[:], in0=ks[:], in1=enb[:], op=ALU.mult), [Bks, Benb], [Bkt])

            def c2():
                for h in range(4):
                    P.tr(pT[:, h, :], qt[:, h * 128:(h + 1) * 128], identb[:], [Bqt, Bidb], [BpT])
                for h in range(4):
                    P.tr(pT[:, 4 + h, :], kt[:, h * 128:(h + 1) * 128], identb[:], [Bkt, Bidb], [BpT])
                P.op("act", lambda e: e.copy(out=qkT[:], in_=pT[:]), [BpT], [BqkT])
                for h in range(4):
                    P.mm(pA[:, h * 128:(h + 1) * 128], qkT[:, 4 + h, :], qkT[:, h, :], True, True, [BqkT], [BpA])
                for h in range(4):
                    P.op("dve", lambda e, h=h: e.tensor_tensor(out=AT[:, h, :], in0=pA[:, h * 128:(h + 1) * 128], in1=m01,
                                                              op=ALU.mult), [BpA, Bcst], [BAT])
                if samp:
                    for s_ in range(4):
                        P.op("dve", lambda e, s_=s_: e.tensor_copy(out=qmT[:, :, s_, 8 * s_:8 * s_ + 8],
                                                                    in_=qkT[:, 0:4, 8 * s_:8 * s_ + 8]), [BqkT], [BqmT])
                        P.op("pool", lambda e, s_=s_: e.tensor_scalar(out=ktm[:, s_, :], in0=kt[:], scalar1=cst[:, 6, s_:s_ + 1],
                                                                       scalar2=None, op0=ALU.mult), [Bkt, Bcst], [Bktm])

            def c3():
                init_state()
                for h in range(4):
                    bank = pO[h // 2]; Bb = BpO[h // 2]
                    oc = (h % 2) * 256
                    for s_ in range(nseg):
                        lhs = qmT[:, h, s_, :] if samp else qkT[:, h, :]
                        P.mm(bank[:, oc:oc + 256], lhs, Sb[:, s_ * 4 + h, :], s_ == 0, False,
                             [BqmT if samp else BqkT, BSb], [Bb])
                    P.mm(bank[:, oc:oc + 256], AT[:, h, :], vb[:, h * 256:(h + 1) * 256], False, True, [BAT, Bvb], [Bb])
                for s_ in range(nseg):
                    for h in range(4):
                        bank = pZ if h < 2 else pA
                        Bb = BpZ if h < 2 else BpA
                        oc = (h % 2) * 256
                        lhs = ktm[:, s_, h * 128:(h + 1) * 128] if samp else kt[:, h * 128:(h + 1) * 128]
                        P.mm(bank[:, oc:oc + 256], lhs, vb[:, h * 256:(h + 1) * 256], True, True,
                             [Bktm if samp else Bkt, Bvb], [Bb])
                    for h in range(4):
                        bank = pZ if h < 2 else pA
                        Bb = BpZ if h < 2 else BpA
                        oc = (h % 2) * 256
                        si = s_ * 4 + h
                        dcol = dec[:, h * nseg + s_:h * nseg + s_ + 1]
                        P.op("dve", lambda e, si=si, dcol=dcol: e.tensor_scalar(out=Stmp[:], in0=S[:, si, :], scalar1=dcol,
                                                                              scalar2=None, op0=ALU.mult), [BS, Bdec], [BStmp])
                        P.op("dve", lambda e, si=si, dcol=dcol, bank=bank, oc=oc: e.scalar_tensor_tensor(
                            out=S[:, si, :], in0=bank[:, oc:oc + 256], scalar=dcol, in1=Stmp[:], op0=ALU.mult, op1=ALU.add),
                            [Bb, Bdec, BStmp], [BS])
                P.op("act", lambda e: e.copy(out=Sb[:, 0:4 * nseg, :], in_=S[:, 0:4 * nseg, :]), [BS], [BSb])
                if t == 15 or t == 31:
                    P.store(C.gla_p[t // 16].rearrange("h d e -> d h e"), S[:, 0:4, :], BS)
                if samp:
                    P.store(C.gla_s.rearrange("s h d e -> d (s h) e"), S[:], BS)

            def c4():
                for h in range(4):
                    bank = pO[h // 2]; Bb = BpO[h // 2]
                    oc = (h % 2) * 256
                    P.op("act", lambda e, h=h, bank=bank, oc=oc: e.activation(out=junk[:, h * 256:(h + 1) * 256], in_=bank[:, oc:oc + 256],
                                                                            func=AF.Square, accum_out=ss2[:, h:h + 1]),
                         [Bb], [Bjunk, Bss2])
                rstd_ops(P, Bsm, ss2[:, 0:4], 4, rs2[:, 0:4], ms2[:, 0:4], neghalf[:, 0:4], Bss2, Brs2, Bms2, 256)
                for h in range(4):
                    bank = pO[h // 2]; Bb = BpO[h // 2]
                    oc = (h % 2) * 256
                    P.op("dve", lambda e, h=h, bank=bank, oc=oc: e.scalar_tensor_tensor(
                        out=on[:, h * 256:(h + 1) * 256], in0=bank[:, oc:oc + 256], scalar=rs2[:, h:h + 1], in1=gon[:],
                        op0=ALU.mult, op1=ALU.mult), [Bb, Brs2, Bgon], [Bon])
                P.op("dve", lambda e: e.tensor_tensor(out=og[:], in0=on[:], in1=sr[:], op=ALU.mult), [Bon, Bsr], [Bog])
                for kc in range(8):
                    P.tr(pT[:, kc, :], og[:, kc * 128:(kc + 1) * 128], identb[:], [Bog, Bidb], [BpT])
                P.op("act", lambda e: e.copy(out=ogT[:], in_=pT[:]), [BpT], [BogT])

            def c5():
                for c in range(2):
                    for kc in range(8):
                        P.mm(pO[c][:], ogT[:, kc, :], wo[:, kc, c * 512:(c + 1) * 512], kc == 0, kc == 7, [BogT, Bwo], [BpO[c]])
                for c in range(2):
                    P.op("act", lambda e, c=c: e.activation(out=junk[:, c * 512:(c + 1) * 512], in_=pO[c][:], func=AF.Square,
                                                            accum_out=ss3[:, c:c + 1]), [BpO[c]], [Bjunk, Bss3])
                P.op("dve", lambda e: e.tensor_tensor(out=ss3[:, 2:3], in0=ss3[:, 0:1], in1=ss3[:, 1:2], op=ALU.add), [Bss3], [Bss3])
                rstd_ops(P, Bsm, ss3[:, 2:3], 1, rs3[:, 0:1], ms3[:, 0:1], neghalf[:, 0:1], Bss3, Brs3, Bms3, D)
                H = h1[t % 2]; BH = Bh1[t % 2]
                for c in range(2):
                    P.op("dve", lambda e, c=c: e.scalar_tensor_tensor(out=yt[:, c * 512:(c + 1) * 512], in0=pO[c][:], scalar=rs3[:, 0:1],
                                                                      in1=g1[:, c * 512:(c + 1) * 512], op0=ALU.mult, op1=ALU.mult),
                         [BpO[c], Brs3, Bg1], [Byt])
                P.op("dve", lambda e: e.tensor_tensor(out=H[:], in0=yt[:], in1=X[:], op=ALU.add), [Byt, BX], [BH])
                P.store(C.H1[t * 128:(t + 1) * 128, :], H[:], BH)
            return [c1, c2, c3, c4, c5]

        load_x(0)
        load_x(1)
        load_x(2)
        h10 = h1_chunks(0)
        pump(8)
        h10[0](); h10[1]()
        pump(16)
        h10[2]()
        pump(8)
        h10[3]()
        pump(len(wl_))
        for c in h1_chunks(1):
            c()
        cur = h2_chunks(0)
        cur[0](); cur[1]()
        for t in range(NT):
            nxt = h2_chunks(t + 1) if t + 1 < NT else None
            h1n = h1_chunks(t + 2) if t + 2 < NT else [lambda: None] * 4
            cur[2]()
            if nxt: nxt[0]()
            h1n[0]()
            cur[3]()
            h1n[1]()
            if nxt: nxt[1]()
            h1n[2]()
            cur[4]()
            h1n[3]()
            if t + 3 < NT:
                load_x(t + 3)
            cur = nxt
        P.emit()


def phase_ffn(C, layer, hin, hout, wgu_pre=None):
    nc = C.nc
    with ExitStack() as st:
        def sb(name, shape, dt):
            return st.enter_context(nc.sbuf_tensor(f"f{layer}_" + name, list(shape), dt))

        def ps(name, shape, dt):
            return st.enter_context(nc.psum_tensor(f"f{layer}_" + name, list(shape), dt))

        P = Phase(nc, f"ffn{layer}")
        P.setup_cast(sb)
        wgu = wgu_pre if wgu_pre is not None else sb("wgu", [128, 8, 2 * DFF], BF16)
        wdn = sb("wdn", [128, 22, D], BF16)
        ga = sb("ga", [128, D], F32); gb = sb("gb", [128, D], F32)
        cst = sb("cst", [128, 128], F32)
        identb = sb("identb", [128, 128], BF16)
        neghalf = sb("neghalf", [128, 8], F32)
        Bwgu, Bwdn, Bga, Bgb, Bcst, Bidb, Bsm = (P.buf(n) for n in ("wgu", "wdn", "ga", "gb", "cst", "identb", "small"))
        ht = [sb(f"ht{i}", [128, 2, D], F32) for i in range(2)]; Bht = P.bufs_n("ht", 2)
        groups = [(2 * i, 2) for i in range(16)] + [(32, 1)]

        def load_h(gi):
            t0, n = groups[gi]
            P.load(ht[gi % 2][:, 0:n, :], hin[t0 * 128:(t0 + n) * 128, :].rearrange("(a p) d -> p a d", p=128), Bht[gi % 2])

        P.load(ga[:], C.gains[layer * 4 + 2], Bga)
        P.load(gb[:], C.gains[layer * 4 + 3], Bgb)
        P.load(cst[:], C.consts[0], Bcst)
        load_h(0)
        load_h(1)
        gu_edges = [0, 1024, 2048, DFF, DFF + 1024, DFF + 2048, 2 * DFF]
        Bgu = P.bufs_n("wgu_blk", 6)
        wq_ = []
        if wgu_pre is None:
            for blk in (0, 3, 1, 4, 2, 5):
                c0, c1 = gu_edges[blk], gu_edges[blk + 1]
                for kc in range(8):
                    wq_.append(lambda kc=kc, c0=c0, c1=c1, blk=blk: P.load_cast(
                        lambda a, b: wgu[:, kc, c0 + a:c0 + b], C.w_gu[layer, kc * 128:(kc + 1) * 128, c0:c1], c1 - c0, Bgu[blk]))
        for j in range(22):
            wq_.append(lambda j=j: P.load_cast(lambda a, b: wdn[:, j, a:b], C.w_dn[layer, j * 128:(j + 1) * 128, :], D, Bwdn))

        def pump(k):
            for _ in range(k):
                if wq_:
                    wq_.pop(0)()

        if wgu_pre is None:
            pump(16)
        P.op("pool", lambda e: e.memset(neghalf[:], -0.5), [], [Bsm])
        P.op("dve", lambda e: e.tensor_copy(out=identb[:], in_=cst[:]), [Bcst], [Bidb])

        junk = sb("junk", [128, D], BF16); Bjunk = P.buf("junk")
        ub = sb("ub", [128, D], BF16); Bub = P.buf("ub")
        uT2 = [sb(f"uT{i}", [128, 8, 256], BF16) for i in range(2)]; BuT2 = P.bufs_n("uT", 2)
        sg = [sb(f"sg{i}", [128, 256], F32) for i in range(3)]; Bsg = P.bufs_n("sg", 3)
        actT = sb("actT", [128, 22, 256], BF16); BactT = P.buf("actT")
        ss = sb("ss", [128, 8], F32); ms = sb("ms", [128, 8], F32); rs = sb("rs", [128, 8], F32)
        Bss, Bms, Brs = P.buf("ss"), P.buf("ms"), P.buf("rs")
        ss3 = sb("ss3", [128, 8], F32); ms3 = sb("ms3", [128, 8], F32); rs3 = sb("rs3", [128, 8], F32)
        Bss3, Bms3, Brs3 = P.buf("ss3"), P.buf("ms3"), P.buf("rs3")
        ho = [sb(f"ho{i}", [128, D], F32) for i in range(2)]; Bho = P.bufs_n("ho", 2)
        pT = ps("pT", [128, 8, 128], BF16); BpT = P.buf("pT")
        pG = [ps(f"pG{i}", [128, 512], F32) for i in range(3)]; BpG = P.bufs_n("pG", 3)
        pY = [ps(f"pY{i}", [128, 512], F32) for i in range(4)]; BpY = P.bufs_n("pY", 4)

        def stage_norm(gi):
            t0, n = groups[gi]
            Ht = ht[gi % 2]; BH = Bht[gi % 2]
            uT = uT2[gi % 2]; BuT = BuT2[gi % 2]
            for a in range(n):
                P.op("act", lambda e, a=a: e.activation(out=junk[:], in_=Ht[:, a, :], func=AF.Square,
                                                      accum_out=ss[:, a:a + 1]), [BH], [Bjunk, Bss])
            rstd_ops(P, Bsm, ss[:, 0:n], n, rs[:, 0:n], ms[:, 0:n], neghalf[:, 0:n], Bss, Brs, Bms, D)
            for a in range(n):
                P.op("dve", lambda e, a=a: e.scalar_tensor_tensor(out=ub[:], in0=Ht[:, a, :], scalar=rs[:, a:a + 1], in1=ga[:],
                                                                op0=ALU.mult, op1=ALU.mult), [BH, Brs, Bga], [Bub])
                for kc in range(8):
                    P.tr(pT[:, kc, :], ub[:, kc * 128:(kc + 1) * 128], identb[:], [Bub, Bidb], [BpT])
                P.op("act", lambda e, a=a: e.copy(out=uT[:, :, a * 128:(a + 1) * 128], in_=pT[:]), [BpT], [BuT])

        stage_norm(0)
        ocount = 0
        for gi, (t0, n) in enumerate(groups):
            Ht = ht[gi % 2]; BH = Bht[gi % 2]
            uT = uT2[gi % 2]; BuT = BuT2[gi % 2]
            ntok = n * 128
            for j in range(22):
                if gi == 0:
                    pump((2 if j < 16 else 4) if wgu_pre is None else 1)
                if j == 12 and gi + 1 < len(groups):
                    stage_norm(gi + 1)
                bank = pG[j % 3]; Bb = BpG[j % 3]
                for kc in range(8):
                    P.mm(bank[:, 0:ntok], wgu[:, kc, j * 128:(j + 1) * 128], uT[:, kc, 0:ntok], kc == 0, kc == 7,
                         [Bgu[(j * 128) // 1024], BuT], [Bb])
                for kc in range(8):
                    P.mm(bank[:, 256:256 + ntok], wgu[:, kc, DFF + j * 128:DFF + (j + 1) * 128], uT[:, kc, 0:ntok], kc == 0, kc == 7,
                         [Bgu[3 + (j * 128) // 1024], BuT], [Bb])
                P.op("act", lambda e, j=j, bank=bank, ntok=ntok: e.activation(out=sg[j % 3][:, 0:ntok], in_=bank[:, 0:ntok], func=AF.Silu),
                     [Bb], [Bsg[j % 3]])
                P.op("dve", lambda e, j=j, bank=bank, ntok=ntok: e.tensor_tensor(out=actT[:, j, 0:ntok], in0=bank[:, 256:256 + ntok],
                                                                    in1=sg[j % 3][:, 0:ntok], op=ALU.mult), [Bb, Bsg[j % 3]], [BactT])
            pump(len(wq_))
            for a in range(n):
                t = t0 + a
                yb = [pY[(2 * a) % 4], pY[(2 * a + 1) % 4]]
                Byb = [BpY[(2 * a) % 4], BpY[(2 * a + 1) % 4]]
                for c in range(2):
                    for j in range(22):
                        P.mm(yb[c][:], actT[:, j, a * 128:(a + 1) * 128], wdn[:, j, c * 512:(c + 1) * 512], j == 0, j == 21,
                             [BactT, Bwdn], [Byb[c]])
                for c in range(2):
                    P.op("act", lambda e, c=c, yb=yb: e.activation(out=junk[:, c * 512:(c + 1) * 512], in_=yb[c][:], func=AF.Square,
                                                                 accum_out=ss3[:, c:c + 1]), [Byb[c]], [Bjunk, Bss3])
                P.op("dve", lambda e: e.tensor_tensor(out=ss3[:, 2:3], in0=ss3[:, 0:1], in1=ss3[:, 1:2], op=ALU.add), [Bss3], [Bss3])
                rstd_ops(P, Bsm, ss3[:, 2:3], 1, rs3[:, 0:1], ms3[:, 0:1], neghalf[:, 0:1], Bss3, Brs3, Bms3, D)
                Ho = ho[ocount % 2]; BHo = Bho[ocount % 2]
                ocount += 1
                for c in range(2):
                    P.op("dve", lambda e, c=c, yb=yb, Ho=Ho: e.scalar_tensor_tensor(out=Ho[:, c * 512:(c + 1) * 512], in0=yb[c][:],
                                                                                  scalar=rs3[:, 0:1], in1=gb[:, c * 512:(c + 1) * 512],
                                                                                  op0=ALU.mult, op1=ALU.mult), [Byb[c], Brs3, Bgb], [BHo])
                P.op("pool", lambda e, Ho=Ho, Ht=Ht, a=a: e.tensor_tensor(out=Ho[:], in0=Ho[:], in1=Ht[:, a, :], op=ALU.add),
                     [BHo, BH], [BHo])
                P.store(hout[t * 128:(t + 1) * 128, :], Ho[:], BHo)
            if gi + 2 < len(groups):
                load_h(gi + 2)
        P.emit()


def phase_qkv(C):
    nc = C.nc
    with ExitStack() as st:
        def sb(name, shape, dt):
            return st.enter_context(nc.sbuf_tensor("q_" + name, list(shape), dt))

        def ps(name, shape, dt):
            return st.enter_context(nc.psum_tensor("q_" + name, list(shape), dt))

        P = Phase(nc, "qkv")
        P.setup_cast(sb)
        wkv = sb("wkv", [128, 8, 1536], BF16)
        wq = sb("wq", [128, 8, 3072], BF16)
        gk = sb("gk", [128, D], F32); gq = sb("gq", [128, D], F32)
        cst = sb("cst", [128, 128], F32)
        identb = sb("identb", [128, 128], BF16)
        neghalf = sb("neghalf", [128, 8], F32)
        Bwkv, Bwq, Bgk, Bgq, Bcst, Bidb, Bsm = (P.buf(n) for n in ("wkv", "wq", "gk", "gq", "cst", "identb", "small"))
        Bwqb = P.bufs_n("wq_blk", 3)
        Bwkvb = P.bufs_n("wkv_blk", 3)
        wq_ = []
        for blk in range(3):
            for kc in range(8):
                wq_.append(lambda kc=kc, blk=blk: P.load_cast(
                    lambda a, b: wq[:, kc, blk * 1024 + a:blk * 1024 + b], C.w_q[kc * 128:(kc + 1) * 128, blk * 1024:(blk + 1) * 1024],
                    1024, Bwqb[blk]))
        for blk in range(3):
            for kc in range(8):
                wq_.append(lambda kc=kc, blk=blk: P.load_cast(
                    lambda a, b: wkv[:, kc, blk * 512 + a:blk * 512 + b], C.w_kv[kc * 128:(kc + 1) * 128, blk * 512:(blk + 1) * 512],
                    512, Bwkvb[blk]))

        def pump(k):
            for _ in range(k):
                if wq_:
                    wq_.pop(0)()
        P.load(gk[:], C.gains[8], Bgk)
        P.load(gq[:], C.gains[4], Bgq)
        P.load(cst[:], C.consts[0], Bcst)
        P.op("pool", lambda e: e.memset(neghalf[:], -0.5), [], [Bsm])
        P.op("dve", lambda e: e.tensor_copy(out=identb[:], in_=cst[:]), [Bcst], [Bidb])
        Bcp = P.buf("cachecopy")
        import os as _os
        _fl = _os.environ.get("QKV_SKIP", "")
        for g in range(3):
            W = WINS[g]
            for s_ in range(4):
                if "B" in _fl:
                    continue
                P.dma(lambda e, g=g, s_=s_, W=W: e.dma_start(
                    out=C.win_s[g][s_, 0:W - 8, :].rearrange("(a b) n -> a (b n)", a=8),
                    in_=C.cache[g][s_, 8:W, :].rearrange("(a b) n -> a (b n)", a=8)), Bcp, writes=[Bcp])

        ht = [sb(f"ht{i}", [128, 2, D], F32) for i in range(2)]; Bht = P.bufs_n("ht", 2)
        junk = sb("junk", [128, D], BF16); Bjunk = P.buf("junk")
        ub = sb("ub", [128, D], BF16); Bub = P.buf("ub")
        ukT2 = [sb(f"ukT{i}", [128, 8, 256], BF16) for i in range(2)]; BukT2 = P.bufs_n("ukT", 2)
        uqT2 = [sb(f"uqT{i}", [128, 8, 256], BF16) for i in range(2)]; BuqT2 = P.bufs_n("uqT", 2)
        ss = sb("ss", [128, 8], F32); ms = sb("ms", [128, 8], F32); rs = sb("rs", [128, 8], F32)
        Bss, Bms, Brs = P.buf("ss"), P.buf("ms"), P.buf("rs")
        qst = [sb(f"qst{i}", [128, 24, 256], BF16) for i in range(2)]; Bqst = P.bufs_n("qst", 2)
        kst = [sb(f"kst{i}", [128, 6, 256], BF16) for i in range(2)]; Bkst = P.bufs_n("kst", 2)
        kvo = [sb(f"kvo{i}", [128, 1536], F32) for i in range(2)]; Bkvo = P.bufs_n("kvo", 2)
        vst = [sb(f"vst{i}", [128, 3, 4, 66], BF16) for i in range(2)]; Bvst = P.bufs_n("vst", 2)
        pT = ps("pT", [128, 8, 128], BF16); BpT = P.buf("pT")
        pQ = [ps(f"pQ{i}", [128, 512], F32) for i in range(4)]; BpQ = P.bufs_n("pQ", 4)
        pK = [ps(f"pK{i}", [128, 512], F32) for i in range(3)]; BpK = P.bufs_n("pK", 3)
        for i in range(2):
            P.op("pool", lambda e, i=i: e.memset(vst[i][:], 1.0), [], [Bvst[i]])

        groups = [(2 * i, 2) for i in range(16)] + [(32, 1)]

        def load_h(gi):
            t0, n = groups[gi]
            P.load(ht[gi % 2][:, 0:n, :], C.H2[t0 * 128:(t0 + n) * 128, :].rearrange("(a p) d -> p a d", p=128), Bht[gi % 2])

        def stage_norm(gi):
            t0, n = groups[gi]
            Ht = ht[gi % 2]; BH = Bht[gi % 2]
            for a in range(n):
                P.op("act", lambda e, a=a: e.activation(out=junk[:], in_=Ht[:, a, :], func=AF.Square,
                                                      accum_out=ss[:, a:a + 1]), [BH], [Bjunk, Bss])
            rstd_ops(P, Bsm, ss[:, 0:n], n, rs[:, 0:n], ms[:, 0:n], neghalf[:, 0:n], Bss, Brs, Bms, D)
            for (gg, Bgg, uTt, BuTt) in ((gk, Bgk, ukT2[gi % 2], BukT2[gi % 2]), (gq, Bgq, uqT2[gi % 2], BuqT2[gi % 2])):
                for a in range(n):
                    P.op("dve", lambda e, a=a, gg=gg: e.scalar_tensor_tensor(out=ub[:], in0=Ht[:, a, :], scalar=rs[:, a:a + 1],
                                                                           in1=gg[:], op0=ALU.mult, op1=ALU.mult),
                         [BH, Brs, Bgg], [Bub])
                    for kc in range(8):
                        P.tr(pT[:, kc, :], ub[:, kc * 128:(kc + 1) * 128], identb[:], [Bub, Bidb], [BpT])
                    P.op("act", lambda e, a=a, uTt=uTt: e.copy(out=uTt[:, :, a * 128:(a + 1) * 128], in_=pT[:]), [BpT], [BuTt])

        load_h(0)
        load_h(1)
        pump(8)
        stage_norm(0)
        qcount = 0
        tcount = 0
        for gi, (t0, n) in enumerate(groups):
            Ht = ht[gi % 2]; BH = Bht[gi % 2]
            ukT = ukT2[gi % 2]; BukT = BukT2[gi % 2]
            uqT = uqT2[gi % 2]; BuqT = BuqT2[gi % 2]
            ntok = n * 128
            Qs = qst[gi % 2]; BQs = Bqst[gi % 2]
            Ks = kst[gi % 2]; BKs = Bkst[gi % 2]
            for cp in range(12 if "Q" not in _fl else 0):
                if gi == 0:
                    pump(2 if cp < 8 else 6)
                if cp == 6 and gi + 1 < len(groups):
                    stage_norm(gi + 1)
                bank = pQ[qcount % 4]; Bb = BpQ[qcount % 4]
                qcount += 1
                for j in range(2):
                    ch = cp * 2 + j
                    for kc in range(8):
                        P.mm(bank[:, j * 256:j * 256 + ntok], wq[:, kc, ch * 128:(ch + 1) * 128], uqT[:, kc, 0:ntok], kc == 0, kc == 7,
                             [Bwqb[(ch * 128) // 1024], BuqT], [Bb])
                eng = "act" if cp % 2 == 0 else "dve"
                if eng == "act":
                    P.op("act", lambda e, cp=cp, bank=bank, Qs=Qs, ntok=ntok: e.copy(
                        out=Qs[:, 2 * cp:2 * cp + 2, 0:ntok], in_=bank[:].rearrange("p (j n) -> p j n", j=2)[:, :, 0:ntok]), [Bb], [BQs])
                else:
                    P.op("dve", lambda e, cp=cp, bank=bank, Qs=Qs, ntok=ntok: e.tensor_copy(
                        out=Qs[:, 2 * cp:2 * cp + 2, 0:ntok], in_=bank[:].rearrange("p (j n) -> p j n", j=2)[:, :, 0:ntok]), [Bb], [BQs])
            for q4 in range(4 if "Q" not in _fl else 0):
                P.store(C.QT[q4 * 6:(q4 + 1) * 6, :, t0 * 128:t0 * 128 + ntok].rearrange("c p n -> p c n"), Qs[:, q4 * 6:(q4 + 1) * 6, 0:ntok], BQs)
            for cp in range(3 if "K" not in _fl else 0):
                bank = pQ[qcount % 4]; Bb = BpQ[qcount % 4]
                qcount += 1
                for j in range(2):
                    ch = cp * 2 + j
                    g_, m_ = ch // 2, ch % 2
                    c0 = g_ * 512 + m_ * 128
                    for kc in range(8):
                        P.mm(bank[:, j * 256:j * 256 + ntok], wkv[:, kc, c0:c0 + 128], ukT[:, kc, 0:ntok], kc == 0, kc == 7,
                             [Bwkvb[g_], BukT], [Bb])
                P.op("act", lambda e, cp=cp, bank=bank, Ks=Ks, ntok=ntok: e.copy(
                    out=Ks[:, 2 * cp:2 * cp + 2, 0:ntok], in_=bank[:].rearrange("p (j n) -> p j n", j=2)[:, :, 0:ntok]), [Bb], [BKs])
            if "K" not in _fl:
                P.store(C.KT[:, :, t0 * 128:t0 * 128 + ntok].rearrange("c p n -> p c n"), Ks[:, :, 0:ntok], BKs)
            for a in range(n if "V" not in _fl else 0):
                t = t0 + a
                KVo = kvo[tcount % 2]; BKVo = Bkvo[tcount % 2]
                Vs = vst[tcount % 2]; BVs = Bvst[tcount % 2]
                tcount += 1
                for g in range(3):
                    for kc in range(8):
                        P.mm(pK[g][:], ukT[:, kc, a * 128:(a + 1) * 128], wkv[:, kc, g * 512:(g + 1) * 512], kc == 0, kc == 7,
                             [BukT, Bwkvb[g]], [BpK[g]])
                    if "2" in _fl:
                        pass
                    elif g % 2 == 0:
                        P.op("act", lambda e, g=g, KVo=KVo: e.copy(out=KVo[:, g * 512:(g + 1) * 512], in_=pK[g][:]), [BpK[g]], [BKVo])
                    else:
                        P.op("dve", lambda e, g=g, KVo=KVo: e.tensor_copy(out=KVo[:, g * 512:(g + 1) * 512], in_=pK[g][:]), [BpK[g]], [BKVo])
                    P.op("pool", lambda e, g=g, Vs=Vs, KVo=KVo: e.tensor_copy(
                        out=Vs[:, g, :, 0:64], in_=KVo[:, g * 512 + 256:(g + 1) * 512].rearrange("p (h d) -> p h d", h=4)),
                        [BKVo], [BVs])
                if "S" not in _fl:
                    P.store(C.VA[:, t * 128:(t + 1) * 128, :].rearrange("g p n -> p g n"), Vs[:].rearrange("p g h d -> p g (h d)"), BVs)
                if t < 32 and "W" not in _fl:
                    sq, tt = t // 16, t % 16
                    for g in range(3):
                        nt_w = WINS[g] // 128
                        if tt >= 16 - nt_w:
                            r0 = (tt - (16 - nt_w)) * 128
                            P.store(C.win_p[g][sq, r0:r0 + 128, :], KVo[:, g * 512:(g + 1) * 512], BKVo)
                elif "A" not in _fl:
                    for g in range(3):
                        W = WINS[g]
                        for s_ in range(4):
                            P.store(C.win_s[g][s_, W - 8:W, :], KVo[s_ * 8:s_ * 8 + 8, g * 512:(g + 1) * 512], BKVo)
            if gi + 2 < len(groups):
                load_h(gi + 2)
        P.emit()


def phase_attn(C):
    nc = C.nc
    with ExitStack() as st:
        def sb(name, shape, dt):
            return st.enter_context(nc.sbuf_tensor("a_" + name, list(shape), dt))

        def ps(name, shape, dt):
            return st.enter_context(nc.psum_tensor("a_" + name, list(shape), dt))

        P = Phase(nc, "attn")
        E = sb("E", [128, 12, 1024], BF16); BE = P.buf("E")
        bt = [sb(f"bt{i}", [128, 1024], F32) for i in range(2)]; Bbt = P.bufs_n("bt", 2)
        cst = sb("cst", [128, 128], F32); Bcst = P.buf("cst")
        identb = sb("identb", [128, 128], BF16); Bidb = P.buf("identb")
        P.load(cst[:], C.consts[0], Bcst)
        P.op("dve", lambda e: e.tensor_copy(out=identb[:], in_=cst[:]), [Bcst], [Bidb])
        for i in range(12):
            P.load(bt[i % 2][:], C.btab[i], Bbt[i % 2])
            P.op("act", lambda e, i=i: e.activation(out=E[:, i, :], in_=bt[i % 2][:], func=AF.Exp), [Bbt[i % 2]], [BE])

        Qb = [sb(f"Qb{i}", [128, 8, SEQ], BF16) for i in range(2)]; BQb = P.bufs_n("Qb", 2)
        KP = [sb(f"KP{i}", [128, 2, 2, SEQ], BF16) for i in range(2)]; BKP = P.bufs_n("KP", 2)
        Vb = [sb(f"Vb{i}", [128, 16, 264], BF16) for i in range(2)]; BVb = P.bufs_n("Vb", 2)
        Pe = [sb(f"Pe{i}", [128, 1024], BF16) for i in range(3)]; BPe = P.bufs_n("Pe", 3)
        PT = [sb(f"PT{i}", [128, 1024], BF16) for i in range(3)]; BPT = P.bufs_n("PT", 3)
        Ot = [sb(f"Ot{i}", [128, OSLOT], F32) for i in range(2)]; BOt = P.bufs_n("Ot", 2)
        pS = [ps(f"pS{i}", [128, 512], F32) for i in range(4)]; BpS = P.bufs_n("pS", 4)
        pO = [ps(f"pO{i}", [128, 512], F32) for i in range(3)]; BpO = P.bufs_n("pO", 3)
        pTr = ps("pTr", [128, 1024], BF16); BpTr = P.buf("pTr")
        for i in range(2):
            P.op("pool", lambda e, i=i: e.memset(KP[i][64:128, :, 0, :], 0.0), [], [BKP[i]])
            P.op("pool", lambda e, i=i: e.memset(KP[i][0:64, :, 1, :], 0.0), [], [BKP[i]])

        cnt = {"s": 0, "p": 0, "o": 0}

        LAG = 1
        pending = []

        def attn_tile(g, q4_of, keytiles, nq, out_rows_ap):
            Oi = cnt["o"] % 2
            cnt["o"] += 1
            merged = nq == 128 and all(kt[1] == 128 for kt in keytiles)
            nkinds = len(keytiles)
            for m in range(2):
                qap, qreads = q4_of(m)
                for half in range(2):
                    mh = m * 2 + half
                    banks = {}
                    for (kind, nk, kp_of, va_of, kreads, vreads) in keytiles:
                        si = cnt["s"] % 4
                        cnt["s"] += 1
                        banks[kind] = (pS[si], BpS[si])
                        if merged:
                            P.mm(pS[si][:], kp_of(m, half), qap, True, True, qreads + kreads, [BpS[si]])
                        else:
                            for qpk in range(4):
                                P.mm(pS[si][0:nk, qpk * 128:qpk * 128 + nq], kp_of(m, half), qap[:, qpk, :], True, True,
                                     qreads + kreads, [BpS[si]])
                    pi = cnt["p"] % 3
                    cnt["p"] += 1
                    eng = "dve" if (cnt["p"] % 2 == 0) else "pool"
                    ei = g * 4 + mh
                    for (kind, nk, kp_of, va_of, kreads, vreads) in keytiles:
                        bank, Bb = banks[kind]
                        if merged:
                            src = bank[:]; dst = Pe[pi][:, kind * 512:(kind + 1) * 512]
                        else:
                            src = bank[0:nk, :].rearrange("p (q n) -> p q n", q=4)[:, :, 0:nq]
                            dst = Pe[pi][0:nk, kind * 512:(kind + 1) * 512].rearrange("p (q n) -> p q n", q=4)[:, :, 0:nq]
                        P.op("act", lambda e, src=src, dst=dst: e.activation(out=dst, in_=src, func=AF.Exp, scale=0.125), [Bb], [BPe[pi]])
                        if not merged:
                            ein = E[0:nk, ei, kind * 512:(kind + 1) * 512].rearrange("p (q n) -> p q n", q=4)[:, :, 0:nq]
                            dst2 = PT[pi][0:nk, kind * 512:(kind + 1) * 512].rearrange("p (q n) -> p q n", q=4)[:, :, 0:nq]
                            P.op(eng, lambda e, dst=dst, ein=ein, dst2=dst2: e.tensor_tensor(out=dst2, in0=dst, in1=ein, op=ALU.mult),
                                 [BPe[pi], BE], [BPT[pi]])
                    if merged:
                        w = 512 * nkinds
                        for (eng_, c0, c1) in (("dve", 0, (w * 5) // 8), ("pool", (w * 5) // 8, w)):
                            P.op(eng_, lambda e, pi=pi, c0=c0, c1=c1, ei=ei: e.tensor_tensor(out=PT[pi][:, c0:c1], in0=Pe[pi][:, c0:c1],
                                                                                              in1=E[:, ei, c0:c1], op=ALU.mult),
                                 [BPe[pi], BE], [BPT[pi]])

                    def pv(m=m, half=half, pi=pi, last=(mh == 3)):
                        kvh = 2 * m + half
                        for qpk in range(4):
                            h = kvh * 4 + qpk
                            ob = pO[h // 7]; Bob = BpO[h // 7]
                            oc = (h % 7) * 65
                            for ki, (kind, nk, kp_of, va_of, kreads, vreads) in enumerate(keytiles):
                                P.mm(ob[0:nq, oc:oc + 65], PT[pi][0:nk, kind * 512 + qpk * 128:kind * 512 + qpk * 128 + nq], va_of(kvh),
                                     ki == 0, ki == nkinds - 1, [BPT[pi]] + vreads, [Bob])
                        if last:
                            O = Ot[Oi]; BO = BOt[Oi]
                            P.op("act", lambda e, O=O: e.copy(out=O[0:nq, 0:455], in_=pO[0][0:nq, 0:455]), [BpO[0]], [BO])
                            P.op("dve", lambda e, O=O: e.tensor_copy(out=O[0:nq, 455:910], in_=pO[1][0:nq, 0:455]), [BpO[1]], [BO])
                            P.op("act", lambda e, O=O: e.copy(out=O[0:nq, 910:1040], in_=pO[2][0:nq, 0:130]), [BpO[2]], [BO])
                            P.store(out_rows_ap, O[0:nq, :], BO)
                    pending.append(pv)
                    if len(pending) > LAG:
                        pending.pop(0)()

        def attn_tile_small(g, qc_of, keytiles, nq, og_rows):
            Oi = cnt["o"] % 2
            cnt["o"] += 1
            nkinds = len(keytiles)
            w = 4 * nq
            for m in range(2):
                qap, qreads = qc_of(m)
                for half in range(2):
                    mh = m * 2 + half
                    ei = g * 4 + mh
                    banks = {}
                    for (kind, nk, kp_of, va_of, kreads, vreads) in keytiles:
                        si = cnt["s"] % 4
                        cnt["s"] += 1
                        banks[kind] = (pS[si], BpS[si])
                        P.mm(pS[si][0:nk, 0:w], kp_of(m, half), qap, True, True, qreads + kreads, [BpS[si]])
                    pi = cnt["p"] % 3
                    cnt["p"] += 1
                    eng = "dve" if (cnt["p"] % 2 == 0) else "pool"
                    for (kind, nk, kp_of, va_of, kreads, vreads) in keytiles:
                        bank, Bb = banks[kind]
                        dst = Pe[pi][0:nk, kind * 512:kind * 512 + w]
                        P.op("act", lambda e, bank=bank, dst=dst, nk=nk: e.activation(out=dst, in_=bank[0:nk, 0:w], func=AF.Exp, scale=0.125),
                             [Bb], [BPe[pi]])
                        ein = E[0:nk, ei, kind * 512:(kind + 1) * 512].rearrange("p (q n) -> p q n", q=4)[:, :, 0:nq]
                        d3 = dst.rearrange("p (q n) -> p q n", q=4)
                        dst2 = PT[pi][0:nk, kind * 512:kind * 512 + w].rearrange("p (q n) -> p q n", q=4)
                        P.op(eng, lambda e, d3=d3, ein=ein, dst2=dst2: e.tensor_tensor(out=dst2, in0=d3, in1=ein, op=ALU.mult),
                             [BPe[pi], BE], [BPT[pi]])

                    def pv(m=m, half=half, pi=pi, last=(mh == 3)):
                        kvh = 2 * m + half
                        for ki, (kind, nk, kp_of, va_of, kreads, vreads) in enumerate(keytiles):
                            P.mm(pO[0][0:w, kvh * 65:kvh * 65 + 65], PT[pi][0:nk, kind * 512:kind * 512 + w], va_of(kvh),
                                 ki == 0, ki == nkinds - 1, [BPT[pi]] + vreads, [BpO[0]])
                        if last:
                            O = Ot[Oi]; BO = BOt[Oi]
                            P.op("act", lambda e, O=O: e.copy(out=O[0:w, 0:260], in_=pO[0][0:w, 0:260]), [BpO[0]], [BO])
                            og3 = og_rows.rearrange("r (h d) -> r h d", d=65)
                            for qpk in range(4):
                                P.store(og3[:, qpk:16:4, :], O[qpk * nq:(qpk + 1) * nq, 0:260].rearrange("r (k d) -> r k d", d=65), BO)
                    pending.append(pv)
                    if len(pending) > LAG:
                        pending.pop(0)()

        def attn_flush():
            while pending:
                pending.pop(0)()

        fill = sb("fill", [96, OSLOT], F32); Bfill = P.buf("fill")
        P.op("pool", lambda e: e.memset(fill[:], 1.0), [], [Bfill])
        for g in range(3):
            P.store(C.OG[g, 4128:4224, :], fill[:], Bfill)
        Qn = sb("Qn", [128, 24, 32], BF16); BQn = P.buf("Qn")
        KPn = sb("KPn", [128, 6, 2, 32], BF16); BKPn = P.buf("KPn")
        P.op("pool", lambda e: e.memset(KPn[:], 0.0), [], [BKPn])
        P.load(Qn[:], C.QT[:, :, 4096:4128].rearrange("c p n -> p c n"), BQn)
        P.load(KPn[0:64, :, 0, :], C.KT[:, 0:64, 4096:4128].rearrange("c p n -> p c n"), BKPn)
        P.load(KPn[64:128, :, 1, :], C.KT[:, 64:128, 4096:4128].rearrange("c p n -> p c n"), BKPn)
        cr = [sb(f"cr{i}", [128, 512], F32) for i in range(2)]; Bcr = P.bufs_n("cr", 2)
        cb = [sb(f"cb{i}", [128, 256], BF16) for i in range(2)]; Bcb = P.bufs_n("cb", 2)
        KPs = [sb(f"KPs{i}", [128, 2, 2, 128], BF16) for i in range(2)]; BKPs = P.bufs_n("KPs", 2)
        VAs = [sb(f"VAs{i}", [128, 4, 66], BF16) for i in range(2)]; BVAs = P.bufs_n("VAs", 2)
        VAn = [sb(f"VAn{i}", [8, 264], BF16) for i in range(2)]; BVAn = P.bufs_n("VAn", 2)
        Qc = [sb(f"Qc{i}", [128, 2, 32], BF16) for i in range(2)]; BQc = P.bufs_n("Qc", 2)
        pTb = pTr
        for i in range(2):
            P.op("pool", lambda e, i=i: e.memset(KPs[i][:], 0.0), [], [BKPs[i]])
            P.op("pool", lambda e, i=i: e.memset(VAs[i][:], 1.0), [], [BVAs[i]])
        sample_jobs = []

        def sample_group(s_, g, rho, i2):
            Dg = DILS[g]
            n = (8 - rho + Dg - 1) // Dg
            W = WINS[g]
            rows = C.cache[g][s_, rho:W:Dg, :]
            P.load(cr[i2][:], rows, Bcr[i2])
            P.op("dve", lambda e: e.tensor_copy(out=cb[i2][:], in_=cr[i2][:, 0:256]), [Bcr[i2]], [Bcb[i2]])
            P.op("pool", lambda e: e.tensor_copy(out=VAs[i2][:, :, 0:64],
                                                 in_=cr[i2][:, 256:512].rearrange("p (h d) -> p h d", h=4)),
                 [Bcr[i2]], [BVAs[i2]])
            for m in range(2):
                P.tr(pTb[:, m * 128:(m + 1) * 128], cb[i2][:, m * 128:(m + 1) * 128], identb[:], [Bcb[i2], Bidb], [BpTr])
            P.op("act", lambda e: e.copy(out=KPs[i2][0:64, :, 0, :], in_=pTb[0:64, 0:256].rearrange("p (m n) -> p m n", m=2)),
                 [BpTr], [BKPs[i2]])
            P.op("dve", lambda e: e.tensor_copy(out=KPs[i2][64:128, :, 1, :],
                                                in_=pTb[64:128, 0:256].rearrange("p (m n) -> p m n", m=2)),
                 [BpTr], [BKPs[i2]])
            tk = 4096 + s_ * 8 + rho
            vsrc = C.VA[g, tk:tk + Dg * (n - 1) + 1:Dg, :]
            P.load(VAn[i2][0:n, :], vsrc, BVAn[i2])
            lc = slice(s_ * 8 + rho, s_ * 8 + rho + Dg * (n - 1) + 1, Dg)
            P.op("dve", lambda e: e.tensor_copy(out=Qc[i2][:, :, 0:4 * n].rearrange("p m (q n) -> p m q n", q=4),
                                                in_=Qn[:, g * 8:(g + 1) * 8, lc].rearrange("p (m q) n -> p m q n", m=2)),
                 [BQn], [BQc[i2]])
            q_of = lambda m: (Qc[i2][:, m, 0:4 * n], [BQc[i2]])
            kts = [
                (1, 128, (lambda m, half: KPs[i2][:, m, half, :]),
                 (lambda kvh: VAs[i2][:, kvh, 0:65]), [BKPs[i2]], [BVAs[i2]]),
                (0, n, (lambda m, half: KPn[:, g * 2 + m, half, lc]),
                 (lambda kvh: VAn[i2][0:n, kvh * 66:kvh * 66 + 65]), [BKPn], [BVAn[i2]]),
            ]
            orow = C.OG[g, tk:tk + Dg * (n - 1) + 1:Dg, :]
            attn_tile_small(g, q_of, kts, n, orow)

        ci = 0
        for s_ in range(4):
            for g in range(3):
                for rho in range(min(DILS[g], 8)):
                    sample_jobs.append((s_, g, rho, ci % 2))
                    ci += 1
        li = 0
        npg = [0]
        for sq in range(2):
            tok0 = sq * SEQ
            for g in range(3):
                Dg = DILS[g]
                ntc = 16 // Dg
                bi = li % 2
                li += 1
                Q, BQ = Qb[bi], BQb[bi]
                K, BK = KP[bi], BKP[bi]
                V, BV = Vb[bi], BVb[bi]
                P.load(Q[:], C.QT[g * 8:(g + 1) * 8, :, tok0:tok0 + SEQ].rearrange("c p n -> p c n"), BQ)
                for m in range(2):
                    P.load(K[0:64, m, 0, :], C.KT[g * 2 + m, 0:64, tok0:tok0 + SEQ], BK)
                    P.load(K[64:128, m, 1, :], C.KT[g * 2 + m, 64:128, tok0:tok0 + SEQ], BK)
                for rho in range(Dg):
                    src = C.VA[g, tok0 + rho:tok0 + SEQ:Dg, :].rearrange("(t c) n -> c t n", c=128)
                    P.load(V[:, rho * ntc:(rho + 1) * ntc, :], src, BV)
                for rho in range(Dg):
                    for tt in range(ntc):
                        def cols(t_):
                            a0 = rho + Dg * 128 * t_
                            return slice(a0, a0 + Dg * 127 + 1, Dg)
                        q_of = lambda m, Q=Q, BQ=BQ, c=cols(tt): (Q[:, m * 4:(m + 1) * 4, c], [BQ])
                        kts = []
                        kt_list = [(0, tt)] + ([(1, tt - 1)] if tt > 0 else [])
                        for kind, tk in kt_list:
                            c = cols(tk)
                            ct = rho * ntc + tk
                            kts.append((kind, 128,
                                        (lambda m, half, K=K, c=c: K[:, m, half, c]),
                                        (lambda kvh, V=V, ct=ct: V[:, ct, kvh * 66:kvh * 66 + 65]),
                                        [BK], [BV]))
                        a0 = tok0 + rho + Dg * 128 * tt
                        out_rows = C.OG[g, a0:a0 + Dg * 127 + 1:Dg, :]
                        attn_tile(g, q_of, kts, 128, out_rows)
                        npg[0] += 1
                        if npg[0] % 2 == 0 and sample_jobs:
                            sample_group(*sample_jobs.pop(0))

        while sample_jobs:
            sample_group(*sample_jobs.pop(0))
        attn_flush()
        P.emit()


def phase_oproj(C, wgu_next=None):
    nc = C.nc
    with ExitStack() as st:
        def sb(name, shape, dt):
            return st.enter_context(nc.sbuf_tensor("o_" + name, list(shape), dt))

        def ps(name, shape, dt):
            return st.enter_context(nc.psum_tensor("o_" + name, list(shape), dt))

        P = Phase(nc, "oproj")
        P.setup_cast(sb)
        wo = sb("wo", [128, 8, D], BF16); Bwo = P.buf("wo")
        g1 = sb("g1", [128, D], F32); Bg1 = P.buf("g1")
        cst = sb("cst", [128, 128], F32); Bcst = P.buf("cst")
        identb = sb("identb", [128, 128], BF16); Bidb = P.buf("identb")
        neghalf = sb("neghalf", [128, 8], F32); Bsm = P.buf("small")
        for kc in range(8):
            P.load_cast(lambda a, b, kc=kc: wo[:, kc, a:b], C.w_o_b[kc * 128:(kc + 1) * 128, :], D, Bwo)
        P.load(g1[:], C.gains[5], Bg1)
        P.load(cst[:], C.consts[0], Bcst)
        P.op("pool", lambda e: e.memset(neghalf[:], -0.5), [], [Bsm])
        P.op("dve", lambda e: e.tensor_copy(out=identb[:], in_=cst[:]), [Bcst], [Bidb])
        og3 = [sb(f"og3_{i}", [128, 3, OSLOT], F32) for i in range(3)]; Bog3 = P.bufs_n("og3_", 3)
        ht = [sb(f"ht{i}", [128, D], F32) for i in range(3)]; Bht = P.bufs_n("ht", 3)
        osum = sb("osum", [128, OSLOT], F32); Bosum = P.buf("osum")
        rden = sb("rden", [128, 16], F32); Brden = P.buf("rden")
        og = sb("og", [128, D], BF16); Bog = P.buf("og")
        ogT = [sb(f"ogT{i}", [128, 8, 128], BF16) for i in range(2)]; BogT = P.bufs_n("ogT", 2)
        junk = sb("junk", [128, D], BF16); Bjunk = P.buf("junk")
        ss3 = sb("ss3", [128, 8], F32); ms3 = sb("ms3", [128, 8], F32); rs3 = sb("rs3", [128, 8], F32)
        Bss3, Bms3, Brs3 = P.buf("ss3"), P.buf("ms3"), P.buf("rs3")
        ho = [sb(f"ho{i}", [128, D], F32) for i in range(2)]; Bho = P.bufs_n("ho", 2)
        pT = ps("pT", [128, 8, 128], BF16); BpT = P.buf("pT")
        pY = [ps(f"pY{i}", [128, 512], F32) for i in range(4)]; BpY = P.bufs_n("pY", 4)

        def load_t(t):
            P.load(og3[t % 3][:], C.OG[:, t * 128:(t + 1) * 128, :].rearrange("g p n -> p g n"), Bog3[t % 3])
            P.load(ht[t % 3][:], C.H2[t * 128:(t + 1) * 128, :], Bht[t % 3])

        def stage_a(t):
            O3 = og3[t % 3]; BO3 = Bog3[t % 3]
            P.op("dve", lambda e, O3=O3: e.tensor_tensor(out=osum[:], in0=O3[:, 0, :], in1=O3[:, 1, :], op=ALU.add), [BO3], [Bosum])
            P.op("dve", lambda e, O3=O3: e.tensor_tensor(out=osum[:], in0=osum[:], in1=O3[:, 2, :], op=ALU.add), [BO3, Bosum], [Bosum])
            P.op("dve", lambda e: e.reciprocal(out=rden[:], in_=osum[:].rearrange("p (h n) -> p h n", n=65)[:, :, 64]), [Bosum], [Brden])
            P.op("dve", lambda e: e.tensor_tensor(out=og[:].rearrange("p (h d) -> p h d", h=16),
                                                  in0=osum[:].rearrange("p (h n) -> p h n", n=65)[:, :, 0:64],
                                                  in1=rden[:].unsqueeze(2).broadcast_to([128, 16, 64]), op=ALU.mult),
                 [Bosum, Brden], [Bog])
            for kc in range(8):
                P.tr(pT[:, kc, :], og[:, kc * 128:(kc + 1) * 128], identb[:], [Bog, Bidb], [BpT])
            P.op("act", lambda e, t=t: e.copy(out=ogT[t % 2][:], in_=pT[:]), [BpT], [BogT[t % 2]])

        def stage_b(t):
            Ht = ht[t % 3]; BH = Bht[t % 3]
            yb = [pY[(2 * t) % 4], pY[(2 * t + 1) % 4]]
            Byb = [BpY[(2 * t) % 4], BpY[(2 * t + 1) % 4]]
            for c in range(2):
                for kc in range(8):
                    P.mm(yb[c][:], ogT[t % 2][:, kc, :], wo[:, kc, c * 512:(c + 1) * 512], kc == 0, kc == 7, [BogT[t % 2], Bwo], [Byb[c]])
            for c in range(2):
                P.op("act", lambda e, c=c, yb=yb: e.activation(out=junk[:, c * 512:(c + 1) * 512], in_=yb[c][:], func=AF.Square,
                                                             accum_out=ss3[:, c:c + 1]), [Byb[c]], [Bjunk, Bss3])
            P.op("dve", lambda e: e.tensor_tensor(out=ss3[:, 2:3], in0=ss3[:, 0:1], in1=ss3[:, 1:2], op=ALU.add), [Bss3], [Bss3])
            rstd_ops(P, Bsm, ss3[:, 2:3], 1, rs3[:, 0:1], ms3[:, 0:1], neghalf[:, 0:1], Bss3, Brs3, Bms3, D)
            Ho = ho[t % 2]; BHo = Bho[t % 2]
            for c in range(2):
                P.op("dve", lambda e, c=c, yb=yb, Ho=Ho: e.scalar_tensor_tensor(out=Ho[:, c * 512:(c + 1) * 512], in0=yb[c][:], scalar=rs3[:, 0:1],
                                                                              in1=g1[:, c * 512:(c + 1) * 512], op0=ALU.mult, op1=ALU.mult),
                     [Byb[c], Brs3, Bg1], [BHo])
            P.op("pool", lambda e, Ho=Ho, Ht=Ht: e.tensor_tensor(out=Ho[:], in0=Ho[:], in1=Ht[:], op=ALU.add), [BHo, BH], [BHo])
            P.store(C.H3[t * 128:(t + 1) * 128, :], Ho[:], BHo)

        pre = []
        inflight = []
        if wgu_next is not None:
            Bwn = P.buf("wgu_next")
            for kc in range(8):
                for c0 in range(0, 2 * DFF, 1024):
                    c1 = min(2 * DFF, c0 + 1024)
                    pre.append((kc, c0, c1))

        def prefetch(k):
            while inflight:
                i, kc, c0, c1 = inflight.pop(0)
                stg, Bs = P.stg[i], P.Bstg[i]
                dst = wgu_next[:, kc, c0:c1]
                n = c1 - c0
                if P.ncast % 2 == 0:
                    P.op("act", lambda e, stg=stg, dst=dst, n=n: e.copy(out=dst, in_=stg[:, 0:n]), [Bs], [Bwn])
                else:
                    P.op("dve", lambda e, stg=stg, dst=dst, n=n: e.tensor_copy(out=dst, in_=stg[:, 0:n]), [Bs], [Bwn])
                P.ncast += 1
            for j in range(k):
                if pre:
                    kc, c0, c1 = pre.pop(0)
                    i = (P.ncast + j) % 3
                    P.load(P.stg[i][:, 0:c1 - c0], C.w_gu[1, kc * 128:(kc + 1) * 128, c0:c1], P.Bstg[i])
                    inflight.append((i, kc, c0, c1))

        load_t(0)
        load_t(1)
        stage_a(0)
        for t in range(NT):
            if t + 2 < NT:
                load_t(t + 2)
            prefetch(2)
            if t + 1 < NT:
                stage_a(t + 1)
            stage_b(t)
        while pre or inflight:
            prefetch(2)
        P.emit()


def _t5_buckets(dist):
    d = np.asarray(dist)
    large = 16 + (np.log(np.maximum(d, 1) / 16) / np.log(2048 / 16) * (32 - 16)).astype(np.int64)
    large = np.minimum(large, 31)
    return np.where(d < 16, d, large).astype(np.int32)


def _bias_tables(rel_bias):
    out = np.full((3, 2, 2, 128, 2, 4, 128), -30000.0, np.float32)
    c = np.arange(128)[:, None]
    i = np.arange(128)[None, :]
    for g in range(3):
        bk = _t5_buckets(DILS[g] * np.arange(129))
        for m in range(2):
            for half in range(2):
                for qpk in range(4):
                    hh = (2 * m + half) * 4 + qpk
                    tab = rel_bias[bk, g * 16 + hh]
                    j0 = i - c
                    out[g, m, half, :, 0, qpk, :] = np.where(j0 >= 0, tab[np.clip(j0, 0, 128)], -30000.0)
                    j1 = i - c + 128
                    out[g, m, half, :, 1, qpk, :] = np.where(j1 <= 128, tab[np.clip(j1, 0, 128)], -30000.0)
    return out.reshape(12, 128, 1024)


def _consts():
    cs = np.zeros((8, 128, 128), np.float32)
    j = np.arange(128)[:, None]
    i = np.arange(128)[None, :]
    cs[0] = np.eye(128, dtype=np.float32)
    cs[2] = (j <= i).astype(np.float32)
    cs[1] = cs[2] * (-1.0 / 16.0)
    same = (j // 8 == i // 8) & (j < 32) & (i < 32)
    cs[4] = ((j <= i) & same).astype(np.float32)
    cs[3] = cs[4] * (-1.0 / 16.0)
    cs[5][:, 0] = -1.0 / 16.0
    for s in range(4):
        cs[5][8 * s:8 * s + 8, 1 + s] = -1.0 / 16.0
        cs[6][8 * s:8 * s + 8, s] = 1.0
    return cs


_NC_CACHE = {}


def kernel(x_prompt, x_sample, state_gla, cache_win1, cache_win2, cache_win3, norm_g, w_in_a,
           w_a2, b_a, g_onorm, w_o_a, g_kv, w_kv, w_q_b, w_o_b, rel_bias, w_gate_up, w_down):
    f = lambda a: np.ascontiguousarray(np.asarray(a, dtype=np.float32))
    x_prompt, x_sample, state_gla = f(x_prompt), f(x_sample), f(state_gla)
    caches = [f(cache_win1), f(cache_win2), f(cache_win3)]
    norm_g, rel_bias = f(norm_g), f(rel_bias)
    if "nc" not in _NC_CACHE:
        _NC_CACHE["nc"] = build()
    nc = _NC_CACHE["nc"]
    gains = np.concatenate([norm_g.reshape(8, 1, D), f(g_kv).reshape(1, 1, D)], axis=0)
    gains = np.ascontiguousarray(np.broadcast_to(gains, (9, 128, D)))
    gon = np.ascontiguousarray(np.broadcast_to(f(g_onorm).reshape(1, 256), (128, 256)))
    wq = f(w_q_b)[0].reshape(D, 3, 2, 2, 4, 64).transpose(0, 1, 2, 4, 3, 5).reshape(D, 3072)
    shared = {
        "gains": gains, "gon": gon, "w_in": f(w_in_a)[0], "w_a2": f(w_a2)[0], "b_a": f(b_a).reshape(1, 512),
        "w_o_a": f(w_o_a)[0], "w_kv": f(w_kv), "w_q": np.ascontiguousarray(wq), "w_o_b": f(w_o_b)[0],
        "w_gu": f(w_gate_up), "w_dn": f(w_down), "btab": _bias_tables(rel_bias), "consts": _consts(),
    }
    in_maps = []
    for c in range(NCORES):
        xs = np.zeros((NTOK, D), np.float32)
        xs[0:4096] = x_prompt[2 * c:2 * c + 2].reshape(4096, D)
        xs[4096:4128] = x_sample[4 * c:4 * c + 4].reshape(32, D)
        m = dict(shared)
        m["x"] = xs
        m["state"] = np.ascontiguousarray(state_gla[0, 4 * c:4 * c + 4])
        for g in range(3):
            m[f"cache{g}"] = np.ascontiguousarray(caches[g][4 * c:4 * c + 4].reshape(4, WINS[g], 512))
        in_maps.append(m)
    res = run_bass_kernel_spmd(nc, in_maps, core_ids=list(range(NCORES)))
    R = res.results
    y_prompt = np.concatenate([R[c]["y"][0:4096].reshape(2, SEQ, D) for c in range(NCORES)], axis=0)
    y_sample = np.concatenate([R[c]["y"][4096:4128].reshape(4, 8, D) for c in range(NCORES)], axis=0)
    gla_p = np.concatenate([R[c]["gla_p"] for c in range(NCORES)], axis=0)[None]
    gla_s = np.concatenate([R[c]["gla_s"] for c in range(NCORES)], axis=0)[None]
    wp = [np.concatenate([R[c][f"win{g}_p"] for c in range(NCORES)], axis=0).reshape(16, WINS[g], 2, 4, 64) for g in range(3)]
    ws = [np.concatenate([R[c][f"win{g}_s"] for c in range(NCORES)], axis=0).reshape(32, WINS[g], 2, 4, 64) for g in range(3)]
    if DEBUG:
        kernel.debug = R
    return (y_prompt, y_sample, gla_p, wp[0], wp[1], wp[2], gla_s, ws[0], ws[1], ws[2])
```

```python
import math
import numpy as np
from contextlib import ExitStack
import concourse.bass as bass
import concourse.mybir as mybir
from concourse.bass_utils import run_bass_kernel_spmd

F32 = mybir.dt.float32
BF16 = mybir.dt.bfloat16
AF = mybir.ActivationFunctionType
ALU = mybir.AluOpType

NCORES = 8
D = 1024
NT = 33
NTOK = NT * 128
SEQ = 2048
DFF = 2816
EPS = 1e-6
DILS = (1, 4, 16)
WINS = (128, 512, 2048)
OSLOT = 1040
DEBUG = False
SCR_EXT = False
STRICT = True
PROFILE_LABELS = False
_PHASES = []


def oslot(h):
    return (h // 7) * 512 + (h % 7) * 65


ENGS = ("pe", "act", "dve", "pool", "sp")


class Buf:
    __slots__ = ("name", "last_writer", "readers", "sem", "semval", "strict")

    def __init__(self, name):
        self.name = name
        self.strict = name.startswith("junk")
        self.last_writer = None
        self.readers = []
        self.sem = None
        self.semval = 0


class Op:
    __slots__ = ("eng", "fn", "deps", "is_dma", "marked", "val", "sembuf", "label")

    def __init__(self, eng, fn, is_dma, sembuf):
        self.label = None
        self.eng = eng
        self.fn = fn
        self.deps = []
        self.is_dma = is_dma
        self.marked = False
        self.val = 0
        self.sembuf = sembuf


class Phase:
    semstack = None

    def __init__(self, nc, name):
        self.nc = nc
        self.name = name
        self.ops = []
        self.bufs = []

    def setup_cast(self, sb):
        self.stg = [sb(f"stg{i}", [128, 1024], F32) for i in range(3)]
        self.Bstg = self.bufs_n("stg", 3)
        self.ncast = 0

    def load_cast(self, dst_of, src, ncols, b, step=1024):
        c0 = 0
        while c0 < ncols:
            c1 = min(ncols, c0 + step)
            i = self.ncast % 3
            eng = ("pool", "act", "dve")[self.ncast % 3]
            self.ncast += 1
            stg, Bs = self.stg[i], self.Bstg[i]
            self.load(stg[:, 0:c1 - c0], src[:, c0:c1], Bs)
            dst = dst_of(c0, c1)
            n = c1 - c0
            if eng == "act":
                self.op("act", lambda e, stg=stg, dst=dst, n=n: e.copy(out=dst, in_=stg[:, 0:n]), [Bs], [b])
            else:
                self.op(eng, lambda e, stg=stg, dst=dst, n=n: e.tensor_copy(out=dst, in_=stg[:, 0:n]), [Bs], [b])
            c0 = c1

    def buf(self, name):
        b = Buf(name)
        self.bufs.append(b)
        return b

    def bufs_n(self, name, n):
        return [self.buf(f"{name}{i}") for i in range(n)]

    def _record(self, op, reads, writes):
        deps = {}
        for b in reads:
            w = b.last_writer
            if w is not None:
                deps[id(w)] = (w, True)
        for b in writes:
            w = b.last_writer
            if w is not None and id(w) not in deps:
                deps[id(w)] = (w, b.strict)
            for r in b.readers:
                if id(r) not in deps:
                    deps[id(r)] = (r, False)
        for d, raw in deps.values():
            if d is op:
                continue
            if (not d.is_dma) and (not op.is_dma) and d.eng == op.eng:
                if STRICT:
                    if op.eng == "pe":
                        continue
                elif op.eng == "pool":
                    pass
                elif not (raw and op.eng in ("act", "dve")):
                    continue
            op.deps.append(d)
        for b in reads:
            b.readers.append(op)
        for b in writes:
            b.last_writer = op
            b.readers = []
        if PROFILE_LABELS:
            import sys as _sys
            f = _sys._getframe(1)
            while f is not None and f.f_code.co_name in ("_record", "op", "dma", "load", "store", "mm", "tr", "load_cast", "<lambda>", "proj", "rstd_ops"):
                f = f.f_back
            op.label = f.f_lineno if f is not None else None
        self.ops.append(op)

    def op(self, eng, fn, reads=(), writes=()):
        o = Op(eng, fn, False, None)
        self._record(o, reads, writes)
        return o

    def dma(self, fn, sembuf, reads=(), writes=(), eng="sp"):
        o = Op(eng, fn, True, sembuf)
        self._record(o, reads, writes)
        return o

    def load(self, out, in_, b, eng="sp", extra_reads=()):
        return self.dma(lambda e: e.dma_start(out=out, in_=in_), b, reads=extra_reads, writes=[b], eng=eng)

    def store(self, out, in_, b, extra_writes=()):
        return self.dma(lambda e: e.dma_start(out=out, in_=in_), b, reads=[b], writes=extra_writes)

    def mm(self, out, lhsT, rhs, start, stop, reads, writes):
        return self.op("pe", lambda e: e.matmul(out, lhsT, rhs, start=start, stop=stop), reads, writes)

    def tr(self, out, in_, ident, reads, writes):
        return self.op("pe", lambda e: e.transpose(out=out, in_=in_, identity=ident), reads, writes)

    def emit(self):
        nc = self.nc
        _PHASES.append(self)
        for o in self.ops:
            for d in o.deps:
                if not d.is_dma:
                    d.marked = True
        cnt = {e: 0 for e in ENGS}
        for o in self.ops:
            if o.is_dma:
                o.sembuf.semval += 16
                o.val = o.sembuf.semval
            elif o.marked:
                cnt[o.eng] += 1
                o.val = cnt[o.eng]
        with ExitStack() as st:
            gst = st
            esem = {e: gst.enter_context(nc.semaphore(f"{self.name}_s_{e}")) for e in ENGS if e != "sp"}
            allsems = list(esem.values())
            for b in self.bufs:
                if b.semval > 0:
                    b.sem = gst.enter_context(nc.semaphore(f"{self.name}_d_{b.name}"))
                    allsems.append(b.sem)
            with nc.Block() as cblock:
                @cblock.gpsimd
                def _(e):
                    for sm in allsems:
                        e.sem_clear(sm)
            block = st.enter_context(nc.Block())
            per = {e: [o for o in self.ops if o.eng == e] for e in ENGS}

            def replay(eng_name, eng):
                seen = {}
                for o in per[eng_name]:
                    need = {}
                    for d in o.deps:
                        if d.is_dma:
                            key = ("d", id(d.sembuf)); sem = d.sembuf.sem
                        else:
                            key = ("e", d.eng); sem = esem[d.eng]
                        if seen.get(key, 0) >= d.val:
                            continue
                        if key not in need or need[key][1] < d.val:
                            need[key] = (sem, d.val)
                    for key, (sem, val) in need.items():
                        eng.wait_ge(sem, val)
                        seen[key] = val
                    ins = o.fn(eng)
                    if o.is_dma:
                        ins.then_inc(o.sembuf.sem, 16)
                    elif o.marked:
                        ins.then_inc(esem[o.eng], 1)
                if eng_name == "sp":
                    for b in self.bufs:
                        if b.semval > 0 and seen.get(("d", id(b)), 0) < b.semval:
                            eng.wait_ge(b.sem, b.semval)

            @block.tensor
            def _(e):
                replay("pe", e)

            @block.scalar
            def _(e):
                replay("act", e)

            @block.vector
            def _(e):
                replay("dve", e)

            @block.gpsimd
            def _(e):
                replay("pool", e)

            @block.sync
            def _(e):
                replay("sp", e)


class Ctx:
    pass


def build():
    nc = bass.Bass("TRN2", target_bir_lowering=False)
    C = Ctx()
    C.nc = nc

    def din(name, shape, dt=F32):
        return nc.dram_tensor(name, list(shape), dt, kind="ExternalInput").ap()

    def dout(name, shape, dt=F32):
        return nc.dram_tensor(name, list(shape), dt, kind="ExternalOutput").ap()

    def dscr(name, shape, dt=F32):
        kind = "ExternalOutput" if (SCR_EXT or (DEBUG and name in ("H1", "H2", "H3"))) else "Internal"
        return nc.dram_tensor(name, list(shape), dt, kind=kind).ap()

    C.x = din("x", [NTOK, D])
    C.state = din("state", [4, 4, 128, 256])
    C.cache = [din(f"cache{g}", [4, WINS[g], 512]) for g in range(3)]
    C.gains = din("gains", [9, 128, D])
    C.gon = din("gon", [128, 256])
    C.w_in = din("w_in", [D, 3088])
    C.w_a2 = din("w_a2", [16, 512])
    C.b_a = din("b_a", [1, 512])
    C.w_o_a = din("w_o_a", [D, D])
    C.w_kv = din("w_kv", [D, 1536])
    C.w_q = din("w_q", [D, 3072])
    C.w_o_b = din("w_o_b", [D, D])
    C.w_gu = din("w_gu", [2, D, 2 * DFF])
    C.w_dn = din("w_dn", [2, DFF, D])
    C.btab = din("btab", [12, 128, 1024])
    C.consts = din("consts", [8, 128, 128])
    C.y = dout("y", [NTOK, D])
    C.gla_p = dout("gla_p", [2, 4, 128, 256])
    C.gla_s = dout("gla_s", [4, 4, 128, 256])
    C.win_p = [dout(f"win{g}_p", [2, WINS[g], 512]) for g in range(3)]
    C.win_s = [dout(f"win{g}_s", [4, WINS[g], 512]) for g in range(3)]
    C.H1 = dscr("H1", [NTOK, D])
    C.H2 = dscr("H2", [NTOK, D])
    C.H3 = dscr("H3", [NTOK, D])
    C.QT = dscr("QT", [24, 128, NTOK], BF16)
    C.KT = dscr("KT", [6, 128, NTOK], BF16)
    C.VA = dscr("VA", [3, NTOK, 264], BF16)
    C.OG = dscr("OG", [3, NTOK, OSLOT])
    if DEBUG:
        C.DBG = nc.dram_tensor("DBG", [NTOK, D], F32, kind="ExternalOutput").ap()

    C.semstack = ExitStack()
    Phase.semstack = C.semstack
    with C.semstack:
        phase_gla(C)
        phase_ffn(C, 0, C.H1, C.H2)
        phase_qkv(C)
        phase_attn(C)
        with nc.sbuf_tensor("wgu1_pre", [128, 8, 2 * DFF], BF16) as wgu1:
            phase_oproj(C, wgu1)
            phase_ffn(C, 1, C.H3, C.y, wgu1)
    return nc


def rstd_ops(P, Bnh, ss_ap, n, rs_ap, tmp_ap, neghalf_ap, Bss, Brs, Btmp, dim):
    P.op("dve", lambda e: e.tensor_scalar(out=tmp_ap, in0=ss_ap, scalar1=1.0 / dim, scalar2=EPS,
                                          op0=ALU.mult, op1=ALU.add), [Bss], [Btmp])
    P.op("pool", lambda e: e.tensor_tensor(out=rs_ap, in0=tmp_ap, in1=neghalf_ap, op=ALU.pow), [Btmp, Bnh], [Brs])


def phase_gla(C):
    nc = C.nc
    with ExitStack() as st:
        def sb(name, shape, dt):
            return st.enter_context(nc.sbuf_tensor("g_" + name, list(shape), dt))

        def ps(name, shape, dt):
            return st.enter_context(nc.psum_tensor("g_" + name, list(shape), dt))

        P = Phase(nc, "gla")
        P.setup_cast(sb)
        win = sb("win", [128, 8, 3088], BF16)
        wo = sb("wo", [128, 8, D], BF16)
        wa2 = sb("wa2", [16, 512], F32)
        ba = sb("ba", [1, 512], F32)
        ones1 = sb("ones1", [1, 128], F32)
        neghalf = sb("neghalf", [128, 8], F32)
        cst = sb("cst", [128, 8, 128], F32)
        identb = sb("identb", [128, 128], BF16)
        g0 = sb("g0", [128, D], F32)
        g1 = sb("g1", [128, D], F32)
        gon = sb("gon", [128, 256], F32)
        Bw, Bwo, Bsm, Bcst, Bidb, Bg = P.buf("win"), P.buf("wo"), P.buf("small"), P.buf("cst"), P.buf("identb"), P.buf("gains")
        Bwa2, Bba, Bgon, Bg1 = P.buf("wa2"), P.buf("ba"), P.buf("gon"), P.buf("g1")
        in_edges = [0, 1024, 2048, 3072, 3088]
        Bwin = P.bufs_n("win_blk", 4)
        wl_ = []
        for blk in (0, 1, 3, 2):
            c0, c1 = in_edges[blk], in_edges[blk + 1]
            for kc in range(8):
                wl_.append(lambda kc=kc, c0=c0, c1=c1, blk=blk: P.load_cast(
                    lambda a, b: win[:, kc, c0 + a:c0 + b], C.w_in[kc * 128:(kc + 1) * 128, c0:c1], c1 - c0, Bwin[blk]))
        for kc in range(8):
            wl_.append(lambda kc=kc: P.load_cast(lambda a, b: wo[:, kc, a:b], C.w_o_a[kc * 128:(kc + 1) * 128, :], D, Bwo))

        def pump(k):
            for _ in range(k):
                if wl_:
                    wl_.pop(0)()

        P.load(wa2[:], C.w_a2, Bwa2)
        P.load(ba[:], C.b_a, Bba)
        P.load(cst[:], C.consts.rearrange("c p n -> p c n"), Bcst)
        P.load(g0[:], C.gains[0], Bg)
        P.load(g1[:], C.gains[1], Bg1)
        P.load(gon[:], C.gon, Bgon)
        P.op("pool", lambda e: e.memset(ones1[:], 1.0), [], [Bsm])
        P.op("pool", lambda e: e.memset(neghalf[:], -0.5), [], [Bsm])
        P.op("dve", lambda e: e.tensor_copy(out=identb[:], in_=cst[:, 0, :]), [Bcst], [Bidb])

        xt = [sb(f"xt{i}", [128, D], F32) for i in range(3)]
        Bxt = P.bufs_n("xt", 3)
        junk = sb("junk", [128, D], BF16)
        Bjunk = P.buf("junk")
        ub = sb("ub", [128, D], BF16)
        Bub = P.buf("ub")
        uT = sb("uT", [128, 8, 128], BF16)
        BuT = P.buf("uT")
        qs2 = [sb(f"qs{i}", [128, 512], F32) for i in range(2)]; ks2 = [sb(f"ks{i}", [128, 512], F32) for i in range(2)]
        vb2 = [sb(f"vb{i}", [128, D], BF16) for i in range(2)]; sr2 = [sb(f"sr{i}", [128, D], BF16) for i in range(2)]
        spl2 = [sb(f"spl{i}", [128, 512], F32) for i in range(2)]
        Bqs2, Bks2, Bvb2, Bsr2, Bspl2 = (P.bufs_n(n, 2) for n in ("qs", "ks", "vb", "sr", "spl"))
        aT = sb("aT", [16, 128], F32)
        BaT = P.buf("aT")
        e1 = sb("e1", [128, 512], F32)
        eb = sb("eb", [128, 512], F32); enb = sb("enb", [128, 512], F32)
        Be1, Beb, Benb = P.buf("e1"), P.buf("eb"), P.buf("enb")
        dec = sb("dec", [128, 16], F32); Bdec = P.buf("dec")
        qt = sb("qt", [128, 512], BF16); kt = sb("kt", [128, 512], BF16)
        Bqt, Bkt = P.buf("qt"), P.buf("kt")
        ktm = sb("ktm", [128, 4, 512], BF16); Bktm = P.buf("ktm")
        qkT = sb("qkT", [128, 8, 128], BF16); BqkT = P.buf("qkT")
        qmT = sb("qmT", [128, 4, 4, 128], BF16); BqmT = P.buf("qmT")
        AT = sb("AT", [128, 4, 128], BF16); BAT = P.buf("AT")
        S = sb("S", [128, 16, 256], F32); Sb = sb("Sb", [128, 16, 256], BF16)
        BS, BSb = P.buf("S"), P.buf("Sb")
        Stmp = sb("Stmp", [128, 256], F32); BStmp = P.buf("Stmp")
        ss = sb("ss", [128, 8], F32); ms = sb("ms", [128, 8], F32); rs = sb("rs", [128, 8], F32)
        Bss, Bms, Brs = P.buf("ss"), P.buf("ms"), P.buf("rs")
        ss2 = sb("ss2", [128, 8], F32); ms2 = sb("ms2", [128, 8], F32); rs2 = sb("rs2", [128, 8], F32)
        Bss2, Bms2, Brs2 = P.buf("ss2"), P.buf("ms2"), P.buf("rs2")
        ss3 = sb("ss3", [128, 8], F32); ms3 = sb("ms3", [128, 8], F32); rs3 = sb("rs3", [128, 8], F32)
        Bss3, Bms3, Brs3 = P.buf("ss3"), P.buf("ms3"), P.buf("rs3")
        on = sb("on", [128, D], F32); Bon = P.buf("on")
        og = sb("og", [128, D], BF16); Bog = P.buf("og")
        ogT = sb("ogT", [128, 8, 128], BF16); BogT = P.buf("ogT")
        yt = sb("yt", [128, D], F32); Byt = P.buf("yt")
        h1 = [sb(f"h1_{i}", [128, D], F32) for i in range(2)]
        Bh1 = P.bufs_n("h1_", 2)
        pT = ps("pT", [128, 8, 128], BF16); BpT = P.buf("pT")
        pP = [ps(f"pP{i}", [128, 512], F32) for i in range(2)]; BpP = P.bufs_n("pP", 2)
        pS = ps("pS", [128, 512], F32); BpSa = P.buf("pSa"); BpSb = P.buf("pSb")
        pZ = ps("pZ", [128, 512], F32); BpZ = P.buf("pZ")
        pA = ps("pA", [128, 512], F32); BpA = P.buf("pA")
        pO = [ps(f"pO{i}", [128, 512], F32) for i in range(2)]; BpO = P.bufs_n("pO", 2)
        pZb = pZ[:].bitcast(BF16)
        pAb = pA[:].bitcast(BF16)

        P.op("pool", lambda e: e.memset(qmT[:], 0.0), [], [BqmT])

        pYb = [pZ, pA]; BpYb = [BpZ, BpA]

        def load_x(t):
            P.load(xt[t % 3][:], C.x[t * 128:(t + 1) * 128, :], Bxt[t % 3])

        pcount = [0]

        def proj_bank():
            i = pcount[0] % 2
            pcount[0] += 1
            return pP[i], BpP[i]

        def proj(n0, evac):
            bank, Bb = proj_bank()
            for kc in range(8):
                P.mm(bank[:], uT[:, kc, :], win[:, kc, n0:n0 + 512], kc == 0, kc == 7, [BuT, Bwin[n0 // 1024]], [Bb])
            evac(bank, Bb)

        def h1_chunks(t):
            X = xt[t % 3]; BX = Bxt[t % 3]
            i2 = t % 2
            qs, ks, vb, sr, spl = qs2[i2], ks2[i2], vb2[i2], sr2[i2], spl2[i2]
            Bqs, Bks, Bvb, Bsr, Bspl = Bqs2[i2], Bks2[i2], Bvb2[i2], Bsr2[i2], Bspl2[i2]

            def c1():
                P.op("act", lambda e: e.activation(out=junk[:], in_=X[:], func=AF.Square, accum_out=ss[:, 0:1]), [BX], [Bjunk, Bss])
                rstd_ops(P, Bsm, ss[:, 0:1], 1, rs[:, 0:1], ms[:, 0:1], neghalf[:, 0:1], Bss, Brs, Bms, D)
                P.op("dve", lambda e: e.scalar_tensor_tensor(out=ub[:], in0=X[:], scalar=rs[:, 0:1], in1=g0[:],
                                                             op0=ALU.mult, op1=ALU.mult), [BX, Brs, Bg], [Bub])
                for kc in range(8):
                    P.tr(pT[:, kc, :], ub[:, kc * 128:(kc + 1) * 128], identb[:], [Bub, Bidb], [BpT])
                P.op("act", lambda e: e.copy(out=uT[:], in_=pT[:]), [BpT], [BuT])

            def c2():
                proj(0, lambda bank, Bb: P.op("act", lambda e: e.copy(out=qs[:], in_=bank[:]), [Bb], [Bqs]))
                proj(512, lambda bank, Bb: P.op("dve", lambda e: e.tensor_copy(out=ks[:], in_=bank[:]), [Bb], [Bks]))

            def c3():
                proj(1024, lambda bank, Bb: P.op("act", lambda e: e.copy(out=vb[:, 0:512], in_=bank[:]), [Bb], [Bvb]))
                proj(1536, lambda bank, Bb: P.op("dve", lambda e: e.tensor_copy(out=vb[:, 512:1024], in_=bank[:]), [Bb], [Bvb]))
                for kc in range(8):
                    P.mm(pS[0:16, 0:128], win[:, kc, 3072:3088], uT[:, kc, :], kc == 0, kc == 7, [BuT, Bwin[3]], [BpSa])
                P.op("dve", lambda e: e.tensor_copy(out=aT[:], in_=pS[0:16, 0:128]), [BpSa], [BaT])

            def c4():
                bank, Bb = proj_bank()
                P.mm(bank[:], aT[:], wa2[:], True, False, [BaT, Bwa2], [Bb])
                P.mm(bank[:], ones1[:], ba[:], False, True, [Bsm, Bba], [Bb])
                P.op("act", lambda e: e.activation(out=e1[:], in_=bank[:], func=AF.Exp, scale=-1.0), [Bb], [Be1])
                P.op("act", lambda e: e.activation(out=spl[:], in_=e1[:], func=AF.Ln, bias=1.0), [Be1], [Bspl])
                proj(2048, lambda bank, Bb: P.op("act", lambda e: e.activation(out=sr[:, 0:512], in_=bank[:], func=AF.Silu), [Bb], [Bsr]))
                proj(2560, lambda bank, Bb: P.op("act", lambda e: e.activation(out=sr[:, 512:1024], in_=bank[:], func=AF.Silu), [Bb], [Bsr]))
            return [c1, c2, c3, c4]

        def h2_chunks(t):
            samp = t == 32
            nseg = 4 if samp else 1
            X = xt[t % 3]; BX = Bxt[t % 3]
            i2 = t % 2
            qs, ks, vb, sr, spl = qs2[i2], ks2[i2], vb2[i2], sr2[i2], spl2[i2]
            Bqs, Bks, Bvb, Bsr, Bspl = Bqs2[i2], Bks2[i2], Bvb2[i2], Bsr2[i2], Bspl2[i2]
            ucs = cst[:, 3, :] if samp else cst[:, 1, :]
            m01 = cst[:, 4, :] if samp else cst[:, 2, :]
            segneg = cst[:, 5, 1:5] if samp else cst[:, 5, 0:1]

            def init_state():
                if t == 0 or t == 16:
                    P.op("pool", lambda e: e.memset(S[:, 0:4, :], 0.0), [], [BS])
                    P.op("pool", lambda e: e.memset(Sb[:, 0:4, :], 0.0), [], [BSb])
                if samp:
                    P.load(S[:], C.state.rearrange("s h d e -> d (s h) e"), BS)
                    P.op("act", lambda e: e.copy(out=Sb[:], in_=S[:]), [BS], [BSb])

            def c1():
                bbank, Bbb = proj_bank()
                P.mm(bbank[:], ucs, spl[:], True, True, [Bcst, Bspl], [Bbb])
                for h in range(4):
                    P.mm(pS[:, 128 + h * nseg:128 + (h + 1) * nseg], spl[:, h * 128:(h + 1) * 128], segneg, True, True,
                         [Bspl, Bcst], [BpSb])
                P.op("act", lambda e: e.activation(out=eb[:], in_=bbank[:], func=AF.Exp), [Bbb], [Beb])
                P.op("act", lambda e: e.activation(out=enb[:], in_=bbank[:], func=AF.Exp, scale=-1.0), [Bbb], [Benb])
                P.op("act", lambda e: e.activation(out=dec[:, 0:4 * nseg], in_=pS[:, 128:128 + 4 * nseg], func=AF.Exp),
                     [BpSb], [Bdec])
                P.op("dve", lambda e: e.scalar_tensor_tensor(out=qt[:], in0=qs[:], scalar=128.0 ** -0.5, in1=eb[:],
                                                             op0=ALU.mult, op1=ALU.mult), [Bqs, Beb], [Bqt])
                P.op("dve", lambda e: e.tensor_tensor(out=kt[:], in0=ks[:], in1=enb[:], op=ALU.mult), [Bks, Benb], [Bkt])

            def c2():
                for h in range(4):
                    P.tr(pT[:, h, :], qt[:, h * 128:(h + 1) * 128], identb[:], [Bqt, Bidb], [BpT])
                for h in range(4):
                    P.tr(pT[:, 4 + h, :], kt[:, h * 128:(h + 1) * 128], identb[:], [Bkt, Bidb], [BpT])
                P.op("act", lambda e: e.copy(out=qkT[:], in_=pT[:]), [BpT], [BqkT])
                for h in range(4):
                    P.mm(pA[:, h * 128:(h + 1) * 128], qkT[:, 4 + h, :], qkT[:, h, :], True, True, [BqkT], [BpA])
                for h in range(4):
                    P.op("dve", lambda e, h=h: e.tensor_tensor(out=AT[:, h, :], in0=pA[:, h * 128:(h + 1) * 128], in1=m01,
                                                              op=ALU.mult), [BpA, Bcst], [BAT])
                if samp:
                    for s_ in range(4):
                        P.op("dve", lambda e, s_=s_: e.tensor_copy(out=qmT[:, :, s_, 8 * s_:8 * s_ + 8],
                                                                    in_=qkT[:, 0:4, 8 * s_:8 * s_ + 8]), [BqkT], [BqmT])
                        P.op("pool", lambda e, s_=s_: e.tensor_scalar(out=ktm[:, s_, :], in0=kt[:], scalar1=cst[:, 6, s_:s_ + 1],
                                                                       scalar2=None, op0=ALU.mult), [Bkt, Bcst], [Bktm])

            def c3():
                init_state()
                for h in range(4):
                    bank = pO[h // 2]; Bb = BpO[h // 2]
                    oc = (h % 2) * 256
                    for s_ in range(nseg):
                        lhs = qmT[:, h, s_, :] if samp else qkT[:, h, :]
                        P.mm(bank[:, oc:oc + 256], lhs, Sb[:, s_ * 4 + h, :], s_ == 0, False,
                             [BqmT if samp else BqkT, BSb], [Bb])
                    P.mm(bank[:, oc:oc + 256], AT[:, h, :], vb[:, h * 256:(h + 1) * 256], False, True, [BAT, Bvb], [Bb])
                for s_ in range(nseg):
                    for h in range(4):
                        bank = pZ if h < 2 else pA
                        Bb = BpZ if h < 2 else BpA
                        oc = (h % 2) * 256
                        lhs = ktm[:, s_, h * 128:(h + 1) * 128] if samp else kt[:, h * 128:(h + 1) * 128]
                        P.mm(bank[:, oc:oc + 256], lhs, vb[:, h * 256:(h + 1) * 256], True, True,
                             [Bktm if samp else Bkt, Bvb], [Bb])
                    for h in range(4):
                        bank = pZ if h < 2 else pA
                        Bb = BpZ if h < 2 else BpA
                        oc = (h % 2) * 256
                        si = s_ * 4 + h
                        dcol = dec[:, h * nseg + s_:h * nseg + s_ + 1]
                        P.op("dve", lambda e, si=si, dcol=dcol: e.tensor_scalar(out=Stmp[:], in0=S[:, si, :], scalar1=dcol,
                                                                              scalar2=None, op0=ALU.mult), [BS, Bdec], [BStmp])
                        P.op("dve", lambda e, si=si, dcol=dcol, bank=bank, oc=oc: e.scalar_tensor_tensor(
                            out=S[:, si, :], in0=bank[:, oc:oc + 256], scalar=dcol, in1=Stmp[:], op0=ALU.mult, op1=ALU.add),
                            [Bb, Bdec, BStmp], [BS])
                P.op("act", lambda e: e.copy(out=Sb[:, 0:4 * nseg, :], in_=S[:, 0:4 * nseg, :]), [BS], [BSb])
                if t == 15 or t == 31:
                    P.store(C.gla_p[t // 16].rearrange("h d e -> d h e"), S[:, 0:4, :], BS)
                if samp:
                    P.store(C.gla_s.rearrange("s h d e -> d (s h) e"), S[:], BS)

            def c4():
                for h in range(4):
                    bank = pO[h // 2]; Bb = BpO[h // 2]
                    oc = (h % 2) * 256
                    P.op("act", lambda e, h=h, bank=bank, oc=oc: e.activation(out=junk[:, h * 256:(h + 1) * 256], in_=bank[:, oc:oc + 256],
                                                                            func=AF.Square, accum_out=ss2[:, h:h + 1]),
                         [Bb], [Bjunk, Bss2])
                rstd_ops(P, Bsm, ss2[:, 0:4], 4, rs2[:, 0:4], ms2[:, 0:4], neghalf[:, 0:4], Bss2, Brs2, Bms2, 256)
                for h in range(4):
                    bank = pO[h // 2]; Bb = BpO[h // 2]
                    oc = (h % 2) * 256
                    P.op("dve", lambda e, h=h, bank=bank, oc=oc: e.scalar_tensor_tensor(
                        out=on[:, h * 256:(h + 1) * 256], in0=bank[:, oc:oc + 256], scalar=rs2[:, h:h + 1], in1=gon[:],
                        op0=ALU.mult, op1=ALU.mult), [Bb, Brs2, Bgon], [Bon])
                P.op("dve", lambda e: e.tensor_tensor(out=og[:], in0=on[:], in1=sr[:], op=ALU.mult), [Bon, Bsr], [Bog])
                for kc in range(8):
                    P.tr(pT[:, kc, :], og[:, kc * 128:(kc + 1) * 128], identb[:], [Bog, Bidb], [BpT])
                P.op("act", lambda e: e.copy(out=ogT[:], in_=pT[:]), [BpT], [BogT])

            def c5():
                for c in range(2):
                    for kc in range(8):
                        P.mm(pYb[c][:], ogT[:, kc, :], wo[:, kc, c * 512:(c + 1) * 512], kc == 0, kc == 7, [BogT, Bwo], [BpYb[c]])
                for c in range(2):
                    P.op("act", lambda e, c=c: e.activation(out=junk[:, c * 512:(c + 1) * 512], in_=pYb[c][:], func=AF.Square,
                                                            accum_out=ss3[:, c:c + 1]), [BpYb[c]], [Bjunk, Bss3])
                P.op("dve", lambda e: e.tensor_tensor(out=ss3[:, 2:3], in0=ss3[:, 0:1], in1=ss3[:, 1:2], op=ALU.add), [Bss3], [Bss3])
                rstd_ops(P, Bsm, ss3[:, 2:3], 1, rs3[:, 0:1], ms3[:, 0:1], neghalf[:, 0:1], Bss3, Brs3, Bms3, D)
                H = h1[t % 2]; BH = Bh1[t % 2]
                for c in range(2):
                    P.op("dve", lambda e, c=c: e.scalar_tensor_tensor(out=yt[:, c * 512:(c + 1) * 512], in0=pYb[c][:], scalar=rs3[:, 0:1],
                                                                      in1=g1[:, c * 512:(c + 1) * 512], op0=ALU.mult, op1=ALU.mult),
                         [BpYb[c], Brs3, Bg1], [Byt])
                P.op("dve", lambda e: e.tensor_tensor(out=H[:], in0=yt[:], in1=X[:], op=ALU.add), [Byt, BX], [BH])
                P.store(C.H1[t * 128:(t + 1) * 128, :], H[:], BH)
            return [c1, c2, c3, c4, c5]

        load_x(0)
        load_x(1)
        load_x(2)
        h10 = h1_chunks(0)
        pump(8)
        h10[0](); h10[1]()
        pump(16)
        h10[2]()
        pump(8)
        h10[3]()
        pump(len(wl_))
        for c in h1_chunks(1):
            c()
        cur = h2_chunks(0)
        cur[0](); cur[1]()
        for t in range(NT):
            nxt = h2_chunks(t + 1) if t + 1 < NT else None
            h1n = h1_chunks(t + 2) if t + 2 < NT else [lambda: None] * 4
            cur[2]()
            if nxt: nxt[0]()
            h1n[0]()
            cur[3]()
            h1n[1]()
            if nxt: nxt[1]()
            h1n[2]()
            cur[4]()
            h1n[3]()
            if t + 3 < NT:
                load_x(t + 3)
            cur = nxt
        P.emit()


def phase_ffn(C, layer, hin, hout, wgu_pre=None):
    nc = C.nc
    with ExitStack() as st:
        def sb(name, shape, dt):
            return st.enter_context(nc.sbuf_tensor(f"f{layer}_" + name, list(shape), dt))

        def ps(name, shape, dt):
            return st.enter_context(nc.psum_tensor(f"f{layer}_" + name, list(shape), dt))

        P = Phase(nc, f"ffn{layer}")
        P.setup_cast(sb)
        wgu = wgu_pre if wgu_pre is not None else sb("wgu", [128, 8, 2 * DFF], BF16)
        wdn = sb("wdn", [128, 22, D], BF16)
        ga = sb("ga", [128, D], F32); gb = sb("gb", [128, D], F32)
        cst = sb("cst", [128, 128], F32)
        identb = sb("identb", [128, 128], BF16)
        neghalf = sb("neghalf", [128, 8], F32)
        Bwgu, Bwdn, Bga, Bgb, Bcst, Bidb, Bsm = (P.buf(n) for n in ("wgu", "wdn", "ga", "gb", "cst", "identb", "small"))
        ht = [sb(f"ht{i}", [128, 2, D], F32) for i in range(2)]; Bht = P.bufs_n("ht", 2)
        groups = [(2 * i, 2) for i in range(16)] + [(32, 1)]

        def load_h(gi):
            t0, n = groups[gi]
            P.load(ht[gi % 2][:, 0:n, :], hin[t0 * 128:(t0 + n) * 128, :].rearrange("(a p) d -> p a d", p=128), Bht[gi % 2])

        P.load(ga[:], C.gains[layer * 4 + 2], Bga)
        P.load(gb[:], C.gains[layer * 4 + 3], Bgb)
        P.load(cst[:], C.consts[0], Bcst)
        load_h(0)
        load_h(1)
        gu_edges = [0, 1024, 2048, DFF, DFF + 1024, DFF + 2048, 2 * DFF]
        Bgu = P.bufs_n("wgu_blk", 6)
        wq_ = []
        if wgu_pre is None:
            for blk in (0, 3, 1, 4, 2, 5):
                c0, c1 = gu_edges[blk], gu_edges[blk + 1]
                for kc in range(8):
                    wq_.append(lambda kc=kc, c0=c0, c1=c1, blk=blk: P.load_cast(
                        lambda a, b: wgu[:, kc, c0 + a:c0 + b], C.w_gu[layer, kc * 128:(kc + 1) * 128, c0:c1], c1 - c0, Bgu[blk]))
        for j in range(22):
            wq_.append(lambda j=j: P.load_cast(lambda a, b: wdn[:, j, a:b], C.w_dn[layer, j * 128:(j + 1) * 128, :], D, Bwdn))

        def pump(k):
            for _ in range(k):
                if wq_:
                    wq_.pop(0)()

        if wgu_pre is None:
            pump(16)
        P.op("pool", lambda e: e.memset(neghalf[:], -0.5), [], [Bsm])
        P.op("dve", lambda e: e.tensor_copy(out=identb[:], in_=cst[:]), [Bcst], [Bidb])

        junk = sb("junk", [128, D], BF16); Bjunk = P.buf("junk")
        ub = sb("ub", [128, D], BF16); Bub = P.buf("ub")
        uT2 = [sb(f"uT{i}", [128, 8, 256], BF16) for i in range(2)]; BuT2 = P.bufs_n("uT", 2)
        sg = [sb(f"sg{i}", [128, 256], F32) for i in range(3)]; Bsg = P.bufs_n("sg", 3)
        actT = sb("actT", [128, 22, 256], BF16); BactT = P.buf("actT")
        ss = sb("ss", [128, 8], F32); ms = sb("ms", [128, 8], F32); rs = sb("rs", [128, 8], F32)
        Bss, Bms, Brs = P.buf("ss"), P.buf("ms"), P.buf("rs")
        ss3 = sb("ss3", [128, 8], F32); ms3 = sb("ms3", [128, 8], F32); rs3 = sb("rs3", [128, 8], F32)
        Bss3, Bms3, Brs3 = P.buf("ss3"), P.buf("ms3"), P.buf("rs3")
        ho = [sb(f"ho{i}", [128, D], F32) for i in range(2)]; Bho = P.bufs_n("ho", 2)
        pT = ps("pT", [128, 8, 128], BF16); BpT = P.buf("pT")
        pG = [ps(f"pG{i}", [128, 512], F32) for i in range(3)]; BpG = P.bufs_n("pG", 3)
        pY = [ps(f"pY{i}", [128, 512], F32) for i in range(4)]; BpY = P.bufs_n("pY", 4)

        def stage_norm(gi):
            t0, n = groups[gi]
            Ht = ht[gi % 2]; BH = Bht[gi % 2]
            uT = uT2[gi % 2]; BuT = BuT2[gi % 2]
            for a in range(n):
                P.op("act", lambda e, a=a: e.activation(out=junk[:], in_=Ht[:, a, :], func=AF.Square,
                                                      accum_out=ss[:, a:a + 1]), [BH], [Bjunk, Bss])
            rstd_ops(P, Bsm, ss[:, 0:n], n, rs[:, 0:n], ms[:, 0:n], neghalf[:, 0:n], Bss, Brs, Bms, D)
            for a in range(n):
                P.op("dve", lambda e, a=a: e.scalar_tensor_tensor(out=ub[:], in0=Ht[:, a, :], scalar=rs[:, a:a + 1], in1=ga[:],
                                                                op0=ALU.mult, op1=ALU.mult), [BH, Brs, Bga], [Bub])
                for kc in range(8):
                    P.tr(pT[:, kc, :], ub[:, kc * 128:(kc + 1) * 128], identb[:], [Bub, Bidb], [BpT])
                P.op("act", lambda e, a=a: e.copy(out=uT[:, :, a * 128:(a + 1) * 128], in_=pT[:]), [BpT], [BuT])

        stage_norm(0)
        ocount = 0
        for gi, (t0, n) in enumerate(groups):
            Ht = ht[gi % 2]; BH = Bht[gi % 2]
            uT = uT2[gi % 2]; BuT = BuT2[gi % 2]
            ntok = n * 128
            for j in range(22):
                if gi == 0:
                    pump((2 if j < 16 else 4) if wgu_pre is None else 1)
                if j == 12 and gi + 1 < len(groups):
                    stage_norm(gi + 1)
                bank = pG[j % 3]; Bb = BpG[j % 3]
                for kc in range(8):
                    P.mm(bank[:, 0:ntok], wgu[:, kc, j * 128:(j + 1) * 128], uT[:, kc, 0:ntok], kc == 0, kc == 7,
                         [Bgu[(j * 128) // 1024], BuT], [Bb])
                for kc in range(8):
                    P.mm(bank[:, 256:256 + ntok], wgu[:, kc, DFF + j * 128:DFF + (j + 1) * 128], uT[:, kc, 0:ntok], kc == 0, kc == 7,
                         [Bgu[3 + (j * 128) // 1024], BuT], [Bb])
                P.op("act", lambda e, j=j, bank=bank, ntok=ntok: e.activation(out=sg[j % 3][:, 0:ntok], in_=bank[:, 0:ntok], func=AF.Silu),
                     [Bb], [Bsg[j % 3]])
                P.op("dve", lambda e, j=j, bank=bank, ntok=ntok: e.tensor_tensor(out=actT[:, j, 0:ntok], in0=bank[:, 256:256 + ntok],
                                                                    in1=sg[j % 3][:, 0:ntok], op=ALU.mult), [Bb, Bsg[j % 3]], [BactT])
            pump(len(wq_))
            for a in range(n):
                t = t0 + a
                yb = [pY[(2 * a) % 4], pY[(2 * a + 1) % 4]]
                Byb = [BpY[(2 * a) % 4], BpY[(2 * a + 1) % 4]]
                for c in range(2):
                    for j in range(22):
                        P.mm(yb[c][:], actT[:, j, a * 128:(a + 1) * 128], wdn[:, j, c * 512:(c + 1) * 512], j == 0, j == 21,
                             [BactT, Bwdn], [Byb[c]])
                for c in range(2):
                    P.op("act", lambda e, c=c, yb=yb: e.activation(out=junk[:, c * 512:(c + 1) * 512], in_=yb[c][:], func=AF.Square,
                                                                 accum_out=ss3[:, c:c + 1]), [Byb[c]], [Bjunk, Bss3])
                P.op("dve", lambda e: e.tensor_tensor(out=ss3[:, 2:3], in0=ss3[:, 0:1], in1=ss3[:, 1:2], op=ALU.add), [Bss3], [Bss3])
                rstd_ops(P, Bsm, ss3[:, 2:3], 1, rs3[:, 0:1], ms3[:, 0:1], neghalf[:, 0:1], Bss3, Brs3, Bms3, D)
                Ho = ho[ocount % 2]; BHo = Bho[ocount % 2]
                ocount += 1
                for c in range(2):
                    P.op("dve", lambda e, c=c, yb=yb, Ho=Ho: e.scalar_tensor_tensor(out=Ho[:, c * 512:(c + 1) * 512], in0=yb[c][:],
                                                                                  scalar=rs3[:, 0:1], in1=gb[:, c * 512:(c + 1) * 512],
                                                                                  op0=ALU.mult, op1=ALU.mult), [Byb[c], Brs3, Bgb], [BHo])
                P.op("pool", lambda e, Ho=Ho, Ht=Ht, a=a: e.tensor_tensor(out=Ho[:], in0=Ho[:], in1=Ht[:, a, :], op=ALU.add),
                     [BHo, BH], [BHo])
                P.store(hout[t * 128:(t + 1) * 128, :], Ho[:], BHo)
            if gi + 2 < len(groups):
                load_h(gi + 2)
        P.emit()


def phase_qkv(C):
    nc = C.nc
    with ExitStack() as st:
        def sb(name, shape, dt):
            return st.enter_context(nc.sbuf_tensor("q_" + name, list(shape), dt))

        def ps(name, shape, dt):
            return st.enter_context(nc.psum_tensor("q_" + name, list(shape), dt))

        P = Phase(nc, "qkv")
        P.setup_cast(sb)
        wkv = sb("wkv", [128, 8, 1536], BF16)
        wq = sb("wq", [128, 8, 3072], BF16)
        gk = sb("gk", [128, D], F32); gq = sb("gq", [128, D], F32)
        cst = sb("cst", [128, 128], F32)
        identb = sb("identb", [128, 128], BF16)
        neghalf = sb("neghalf", [128, 8], F32)
        Bwkv, Bwq, Bgk, Bgq, Bcst, Bidb, Bsm = (P.buf(n) for n in ("wkv", "wq", "gk", "gq", "cst", "identb", "small"))
        Bwqb = P.bufs_n("wq_blk", 3)
        Bwkvb = P.bufs_n("wkv_blk", 3)
        wq_ = []
        for blk in range(3):
            for kc in range(8):
                wq_.append(lambda kc=kc, blk=blk: P.load_cast(
                    lambda a, b: wq[:, kc, blk * 1024 + a:blk * 1024 + b], C.w_q[kc * 128:(kc + 1) * 128, blk * 1024:(blk + 1) * 1024],
                    1024, Bwqb[blk]))
        for blk in range(3):
            for kc in range(8):
                wq_.append(lambda kc=kc, blk=blk: P.load_cast(
                    lambda a, b: wkv[:, kc, blk * 512 + a:blk * 512 + b], C.w_kv[kc * 128:(kc + 1) * 128, blk * 512:(blk + 1) * 512],
                    512, Bwkvb[blk]))

        def pump(k):
            for _ in range(k):
                if wq_:
                    wq_.pop(0)()
        P.load(gk[:], C.gains[8], Bgk)
        P.load(gq[:], C.gains[4], Bgq)
        P.load(cst[:], C.consts[0], Bcst)
        P.op("pool", lambda e: e.memset(neghalf[:], -0.5), [], [Bsm])
        P.op("dve", lambda e: e.tensor_copy(out=identb[:], in_=cst[:]), [Bcst], [Bidb])
        Bcp = P.buf("cachecopy")
        import os as _os
        _fl = _os.environ.get("QKV_SKIP", "")
        for g in range(3):
            W = WINS[g]
            for s_ in range(4):
                if "B" in _fl:
                    continue
                P.dma(lambda e, g=g, s_=s_, W=W: e.dma_start(
                    out=C.win_s[g][s_, 0:W - 8, :].rearrange("(a b) n -> a (b n)", a=8),
                    in_=C.cache[g][s_, 8:W, :].rearrange("(a b) n -> a (b n)", a=8)), Bcp, writes=[Bcp])

        ht = [sb(f"ht{i}", [128, 2, D], F32) for i in range(2)]; Bht = P.bufs_n("ht", 2)
        junk = sb("junk", [128, D], BF16); Bjunk = P.buf("junk")
        ub = sb("ub", [128, D], BF16); Bub = P.buf("ub")
        ukT2 = [sb(f"ukT{i}", [128, 8, 256], BF16) for i in range(2)]; BukT2 = P.bufs_n("ukT", 2)
        uqT2 = [sb(f"uqT{i}", [128, 8, 256], BF16) for i in range(2)]; BuqT2 = P.bufs_n("uqT", 2)
        ss = sb("ss", [128, 8], F32); ms = sb("ms", [128, 8], F32); rs = sb("rs", [128, 8], F32)
        Bss, Bms, Brs = P.buf("ss"), P.buf("ms"), P.buf("rs")
        qst = [sb(f"qst{i}", [128, 24, 256], BF16) for i in range(2)]; Bqst = P.bufs_n("qst", 2)
        kst = [sb(f"kst{i}", [128, 6, 256], BF16) for i in range(2)]; Bkst = P.bufs_n("kst", 2)
        kvo = [sb(f"kvo{i}", [128, 1536], F32) for i in range(2)]; Bkvo = P.bufs_n("kvo", 2)
        vst = [sb(f"vst{i}", [128, 3, 4, 66], BF16) for i in range(2)]; Bvst = P.bufs_n("vst", 2)
        pT = ps("pT", [128, 8, 128], BF16); BpT = P.buf("pT")
        pQ = [ps(f"pQ{i}", [128, 512], F32) for i in range(4)]; BpQ = P.bufs_n("pQ", 4)
        pK = [ps(f"pK{i}", [128, 512], F32) for i in range(3)]; BpK = P.bufs_n("pK", 3)
        for i in range(2):
            P.op("pool", lambda e, i=i: e.memset(vst[i][:], 1.0), [], [Bvst[i]])

        groups = [(2 * i, 2) for i in range(16)] + [(32, 1)]

        def load_h(gi):
            t0, n = groups[gi]
            P.load(ht[gi % 2][:, 0:n, :], C.H2[t0 * 128:(t0 + n) * 128, :].rearrange("(a p) d -> p a d", p=128), Bht[gi % 2])

        def stage_norm(gi):
            t0, n = groups[gi]
            Ht = ht[gi % 2]; BH = Bht[gi % 2]
            for a in range(n):
                P.op("act", lambda e, a=a: e.activation(out=junk[:], in_=Ht[:, a, :], func=AF.Square,
                                                      accum_out=ss[:, a:a + 1]), [BH], [Bjunk, Bss])
            rstd_ops(P, Bsm, ss[:, 0:n], n, rs[:, 0:n], ms[:, 0:n], neghalf[:, 0:n], Bss, Brs, Bms, D)
            for (gg, Bgg, uTt, BuTt) in ((gk, Bgk, ukT2[gi % 2], BukT2[gi % 2]), (gq, Bgq, uqT2[gi % 2], BuqT2[gi % 2])):
                for a in range(n):
                    P.op("dve", lambda e, a=a, gg=gg: e.scalar_tensor_tensor(out=ub[:], in0=Ht[:, a, :], scalar=rs[:, a:a + 1],
                                                                           in1=gg[:], op0=ALU.mult, op1=ALU.mult),
                         [BH, Brs, Bgg], [Bub])
                    for kc in range(8):
                        P.tr(pT[:, kc, :], ub[:, kc * 128:(kc + 1) * 128], identb[:], [Bub, Bidb], [BpT])
                    P.op("act", lambda e, a=a, uTt=uTt: e.copy(out=uTt[:, :, a * 128:(a + 1) * 128], in_=pT[:]), [BpT], [BuTt])

        load_h(0)
        load_h(1)
        pump(8)
        stage_norm(0)
        qcount = 0
        tcount = 0
        for gi, (t0, n) in enumerate(groups):
            Ht = ht[gi % 2]; BH = Bht[gi % 2]
            ukT = ukT2[gi % 2]; BukT = BukT2[gi % 2]
            uqT = uqT2[gi % 2]; BuqT = BuqT2[gi % 2]
            ntok = n * 128
            Qs = qst[gi % 2]; BQs = Bqst[gi % 2]
            Ks = kst[gi % 2]; BKs = Bkst[gi % 2]
            for cp in range(12 if "Q" not in _fl else 0):
                if gi == 0:
                    pump(2 if cp < 8 else 6)
                if cp == 6 and gi + 1 < len(groups):
                    stage_norm(gi + 1)
                bank = pQ[qcount % 4]; Bb = BpQ[qcount % 4]
                qcount += 1
                for j in range(2):
                    ch = cp * 2 + j
                    for kc in range(8):
                        P.mm(bank[:, j * 256:j * 256 + ntok], wq[:, kc, ch * 128:(ch + 1) * 128], uqT[:, kc, 0:ntok], kc == 0, kc == 7,
                             [Bwqb[(ch * 128) // 1024], BuqT], [Bb])
                eng = "act" if cp % 2 == 0 else "dve"
                if eng == "act":
                    P.op("act", lambda e, cp=cp, bank=bank, Qs=Qs, ntok=ntok: e.copy(
                        out=Qs[:, 2 * cp:2 * cp + 2, 0:ntok], in_=bank[:].rearrange("p (j n) -> p j n", j=2)[:, :, 0:ntok]), [Bb], [BQs])
                else:
                    P.op("dve", lambda e, cp=cp, bank=bank, Qs=Qs, ntok=ntok: e.tensor_copy(
                        out=Qs[:, 2 * cp:2 * cp + 2, 0:ntok], in_=bank[:].rearrange("p (j n) -> p j n", j=2)[:, :, 0:ntok]), [Bb], [BQs])
            for q4 in range(4 if "Q" not in _fl else 0):
                P.store(C.QT[q4 * 6:(q4 + 1) * 6, :, t0 * 128:t0 * 128 + ntok].rearrange("c p n -> p c n"), Qs[:, q4 * 6:(q4 + 1) * 6, 0:ntok], BQs)
            for cp in range(3 if "K" not in _fl else 0):
                bank = pQ[qcount % 4]; Bb = BpQ[qcount % 4]
                qcount += 1
                for j in range(2):
                    ch = cp * 2 + j
                    g_, m_ = ch // 2, ch % 2
                    c0 = g_ * 512 + m_ * 128
                    for kc in range(8):
                        P.mm(bank[:, j * 256:j * 256 + ntok], wkv[:, kc, c0:c0 + 128], ukT[:, kc, 0:ntok], kc == 0, kc == 7,
                             [Bwkvb[g_], BukT], [Bb])
                P.op("act", lambda e, cp=cp, bank=bank, Ks=Ks, ntok=ntok: e.copy(
                    out=Ks[:, 2 * cp:2 * cp + 2, 0:ntok], in_=bank[:].rearrange("p (j n) -> p j n", j=2)[:, :, 0:ntok]), [Bb], [BKs])
            if "K" not in _fl:
                P.store(C.KT[:, :, t0 * 128:t0 * 128 + ntok].rearrange("c p n -> p c n"), Ks[:, :, 0:ntok], BKs)
            for a in range(n if "V" not in _fl else 0):
                t = t0 + a
                KVo = kvo[tcount % 2]; BKVo = Bkvo[tcount % 2]
                Vs = vst[tcount % 2]; BVs = Bvst[tcount % 2]
                tcount += 1
                for g in range(3):
                    for kc in range(8):
                        P.mm(pK[g][:], ukT[:, kc, a * 128:(a + 1) * 128], wkv[:, kc, g * 512:(g + 1) * 512], kc == 0, kc == 7,
                             [BukT, Bwkvb[g]], [BpK[g]])
                    if "2" in _fl:
                        pass
                    elif g % 2 == 0:
                        P.op("act", lambda e, g=g, KVo=KVo: e.copy(out=KVo[:, g * 512:(g + 1) * 512], in_=pK[g][:]), [BpK[g]], [BKVo])
                    else:
                        P.op("dve", lambda e, g=g, KVo=KVo: e.tensor_copy(out=KVo[:, g * 512:(g + 1) * 512], in_=pK[g][:]), [BpK[g]], [BKVo])
                    P.op("pool", lambda e, g=g, Vs=Vs, KVo=KVo: e.tensor_copy(
                        out=Vs[:, g, :, 0:64], in_=KVo[:, g * 512 + 256:(g + 1) * 512].rearrange("p (h d) -> p h d", h=4)),
                        [BKVo], [BVs])
                if "S" not in _fl:
                    P.store(C.VA[:, t * 128:(t + 1) * 128, :].rearrange("g p n -> p g n"), Vs[:].rearrange("p g h d -> p g (h d)"), BVs)
                if t < 32 and "W" not in _fl:
                    sq, tt = t // 16, t % 16
                    for g in range(3):
                        nt_w = WINS[g] // 128
                        if tt >= 16 - nt_w:
                            r0 = (tt - (16 - nt_w)) * 128
                            P.store(C.win_p[g][sq, r0:r0 + 128, :], KVo[:, g * 512:(g + 1) * 512], BKVo)
                elif "A" not in _fl:
                    for g in range(3):
                        W = WINS[g]
                        for s_ in range(4):
                            P.store(C.win_s[g][s_, W - 8:W, :], KVo[s_ * 8:s_ * 8 + 8, g * 512:(g + 1) * 512], BKVo)
            if gi + 2 < len(groups):
                load_h(gi + 2)
        P.emit()


def phase_attn(C):
    nc = C.nc
    with ExitStack() as st:
        def sb(name, shape, dt):
            return st.enter_context(nc.sbuf_tensor("a_" + name, list(shape), dt))

        def ps(name, shape, dt):
            return st.enter_context(nc.psum_tensor("a_" + name, list(shape), dt))

        P = Phase(nc, "attn")
        E = sb("E", [128, 12, 1024], BF16); BE = P.buf("E")
        bt = [sb(f"bt{i}", [128, 1024], F32) for i in range(2)]; Bbt = P.bufs_n("bt", 2)
        cst = sb("cst", [128, 128], F32); Bcst = P.buf("cst")
        identb = sb("identb", [128, 128], BF16); Bidb = P.buf("identb")
        P.load(cst[:], C.consts[0], Bcst)
        P.op("dve", lambda e: e.tensor_copy(out=identb[:], in_=cst[:]), [Bcst], [Bidb])
        for i in range(12):
            P.load(bt[i % 2][:], C.btab[i], Bbt[i % 2])
            P.op("act", lambda e, i=i: e.activation(out=E[:, i, :], in_=bt[i % 2][:], func=AF.Exp), [Bbt[i % 2]], [BE])

        Qb = [sb(f"Qb{i}", [128, 8, SEQ], BF16) for i in range(2)]; BQb = P.bufs_n("Qb", 2)
        KP = [sb(f"KP{i}", [128, 2, 2, SEQ], BF16) for i in range(2)]; BKP = P.bufs_n("KP", 2)
        Vb = [sb(f"Vb{i}", [128, 16, 264], BF16) for i in range(2)]; BVb = P.bufs_n("Vb", 2)
        Pe = [sb(f"Pe{i}", [128, 1024], BF16) for i in range(3)]; BPe = P.bufs_n("Pe", 3)
        PT = [sb(f"PT{i}", [128, 1024], BF16) for i in range(3)]; BPT = P.bufs_n("PT", 3)
        Ot = [sb(f"Ot{i}", [128, OSLOT], F32) for i in range(2)]; BOt = P.bufs_n("Ot", 2)
        pS = [ps(f"pS{i}", [128, 512], F32) for i in range(4)]; BpS = P.bufs_n("pS", 4)
        pO = [ps(f"pO{i}", [128, 512], F32) for i in range(3)]; BpO = P.bufs_n("pO", 3)
        pTr = ps("pTr", [128, 1024], BF16); BpTr = P.buf("pTr")
        for i in range(2):
            P.op("pool", lambda e, i=i: e.memset(KP[i][64:128, :, 0, :], 0.0), [], [BKP[i]])
            P.op("pool", lambda e, i=i: e.memset(KP[i][0:64, :, 1, :], 0.0), [], [BKP[i]])

        cnt = {"s": 0, "p": 0, "o": 0}

        LAG = 1
        pending = []

        def attn_tile(g, q4_of, keytiles, nq, out_rows_ap):
            Oi = cnt["o"] % 2
            cnt["o"] += 1
            merged = nq == 128 and all(kt[1] == 128 for kt in keytiles)
            nkinds = len(keytiles)
            for m in range(2):
                qap, qreads = q4_of(m)
                for half in range(2):
                    mh = m * 2 + half
                    banks = {}
                    for (kind, nk, kp_of, va_of, kreads, vreads) in keytiles:
                        si = cnt["s"] % 4
                        cnt["s"] += 1
                        banks[kind] = (pS[si], BpS[si])
                        if merged:
                            P.mm(pS[si][:], kp_of(m, half), qap, True, True, qreads + kreads, [BpS[si]])
                        else:
                            for qpk in range(4):
                                P.mm(pS[si][0:nk, qpk * 128:qpk * 128 + nq], kp_of(m, half), qap[:, qpk, :], True, True,
                                     qreads + kreads, [BpS[si]])
                    pi = cnt["p"] % 3
                    cnt["p"] += 1
                    eng = "dve" if (cnt["p"] % 2 == 0) else "pool"
                    ei = g * 4 + mh
                    for (kind, nk, kp_of, va_of, kreads, vreads) in keytiles:
                        bank, Bb = banks[kind]
                        if merged:
                            src = bank[:]; dst = Pe[pi][:, kind * 512:(kind + 1) * 512]
                        else:
                            src = bank[0:nk, :].rearrange("p (q n) -> p q n", q=4)[:, :, 0:nq]
                            dst = Pe[pi][0:nk, kind * 512:(kind + 1) * 512].rearrange("p (q n) -> p q n", q=4)[:, :, 0:nq]
                        P.op("act", lambda e, src=src, dst=dst: e.activation(out=dst, in_=src, func=AF.Exp, scale=0.125), [Bb], [BPe[pi]])
                        if not merged:
                            ein = E[0:nk, ei, kind * 512:(kind + 1) * 512].rearrange("p (q n) -> p q n", q=4)[:, :, 0:nq]
                            dst2 = PT[pi][0:nk, kind * 512:(kind + 1) * 512].rearrange("p (q n) -> p q n", q=4)[:, :, 0:nq]
                            P.op(eng, lambda e, dst=dst, ein=ein, dst2=dst2: e.tensor_tensor(out=dst2, in0=dst, in1=ein, op=ALU.mult),
                                 [BPe[pi], BE], [BPT[pi]])
                    if merged:
                        w = 512 * nkinds
                        for (eng_, c0, c1) in (("dve", 0, (w * 5) // 8), ("pool", (w * 5) // 8, w)):
                            P.op(eng_, lambda e, pi=pi, c0=c0, c1=c1, ei=ei: e.tensor_tensor(out=PT[pi][:, c0:c1], in0=Pe[pi][:, c0:c1],
                                                                                              in1=E[:, ei, c0:c1], op=ALU.mult),
                                 [BPe[pi], BE], [BPT[pi]])

                    def pv(m=m, half=half, pi=pi, last=(mh == 3)):
                        kvh = 2 * m + half
                        for qpk in range(4):
                            h = kvh * 4 + qpk
                            ob = pO[h // 7]; Bob = BpO[h // 7]
                            oc = (h % 7) * 65
                            for ki, (kind, nk, kp_of, va_of, kreads, vreads) in enumerate(keytiles):
                                P.mm(ob[0:nq, oc:oc + 65], PT[pi][0:nk, kind * 512 + qpk * 128:kind * 512 + qpk * 128 + nq], va_of(kvh),
                                     ki == 0, ki == nkinds - 1, [BPT[pi]] + vreads, [Bob])
                        if last:
                            O = Ot[Oi]; BO = BOt[Oi]
                            P.op("act", lambda e, O=O: e.copy(out=O[0:nq, 0:455], in_=pO[0][0:nq, 0:455]), [BpO[0]], [BO])
                            P.op("dve", lambda e, O=O: e.tensor_copy(out=O[0:nq, 455:910], in_=pO[1][0:nq, 0:455]), [BpO[1]], [BO])
                            P.op("act", lambda e, O=O: e.copy(out=O[0:nq, 910:1040], in_=pO[2][0:nq, 0:130]), [BpO[2]], [BO])
                            P.store(out_rows_ap, O[0:nq, :], BO)
                    pending.append(pv)
                    if len(pending) > LAG:
                        pending.pop(0)()

        def attn_tile_small(g, qc_of, keytiles, nq, og_rows):
            Oi = cnt["o"] % 2
            cnt["o"] += 1
            nkinds = len(keytiles)
            w = 4 * nq
            for m in range(2):
                qap, qreads = qc_of(m)
                for half in range(2):
                    mh = m * 2 + half
                    ei = g * 4 + mh
                    banks = {}
                    for (kind, nk, kp_of, va_of, kreads, vreads) in keytiles:
                        si = cnt["s"] % 4
                        cnt["s"] += 1
                        banks[kind] = (pS[si], BpS[si])
                        P.mm(pS[si][0:nk, 0:w], kp_of(m, half), qap, True, True, qreads + kreads, [BpS[si]])
                    pi = cnt["p"] % 3
                    cnt["p"] += 1
                    eng = "dve" if (cnt["p"] % 2 == 0) else "pool"
                    for (kind, nk, kp_of, va_of, kreads, vreads) in keytiles:
                        bank, Bb = banks[kind]
                        dst = Pe[pi][0:nk, kind * 512:kind * 512 + w]
                        P.op("act", lambda e, bank=bank, dst=dst, nk=nk: e.activation(out=dst, in_=bank[0:nk, 0:w], func=AF.Exp, scale=0.125),
                             [Bb], [BPe[pi]])
                        ein = E[0:nk, ei, kind * 512:(kind + 1) * 512].rearrange("p (q n) -> p q n", q=4)[:, :, 0:nq]
                        d3 = dst.rearrange("p (q n) -> p q n", q=4)
                        dst2 = PT[pi][0:nk, kind * 512:kind * 512 + w].rearrange("p (q n) -> p q n", q=4)
                        P.op(eng, lambda e, d3=d3, ein=ein, dst2=dst2: e.tensor_tensor(out=dst2, in0=d3, in1=ein, op=ALU.mult),
                             [BPe[pi], BE], [BPT[pi]])

                    def pv(m=m, half=half, pi=pi, last=(mh == 3)):
                        kvh = 2 * m + half
                        for ki, (kind, nk, kp_of, va_of, kreads, vreads) in enumerate(keytiles):
                            P.mm(pO[0][0:w, kvh * 65:kvh * 65 + 65], PT[pi][0:nk, kind * 512:kind * 512 + w], va_of(kvh),
                                 ki == 0, ki == nkinds - 1, [BPT[pi]] + vreads, [BpO[0]])
                        if last:
                            O = Ot[Oi]; BO = BOt[Oi]
                            P.op("act", lambda e, O=O: e.copy(out=O[0:w, 0:260], in_=pO[0][0:w, 0:260]), [BpO[0]], [BO])
                            og3 = og_rows.rearrange("r (h d) -> r h d", d=65)
                            for qpk in range(4):
                                P.store(og3[:, qpk:16:4, :], O[qpk * nq:(qpk + 1) * nq, 0:260].rearrange("r (k d) -> r k d", d=65), BO)
                    pending.append(pv)
                    if len(pending) > LAG:
                        pending.pop(0)()

        def attn_flush():
            while pending:
                pending.pop(0)()

        fill = sb("fill", [96, OSLOT], F32); Bfill = P.buf("fill")
        P.op("pool", lambda e: e.memset(fill[:], 1.0), [], [Bfill])
        for g in range(3):
            P.store(C.OG[g, 4128:4224, :], fill[:], Bfill)
        Qn = sb("Qn", [128, 24, 32], BF16); BQn = P.buf("Qn")
        KPn = sb("KPn", [128, 6, 2, 32], BF16); BKPn = P.buf("KPn")
        P.op("pool", lambda e: e.memset(KPn[:], 0.0), [], [BKPn])
        P.load(Qn[:], C.QT[:, :, 4096:4128].rearrange("c p n -> p c n"), BQn)
        P.load(KPn[0:64, :, 0, :], C.KT[:, 0:64, 4096:4128].rearrange("c p n -> p c n"), BKPn)
        P.load(KPn[64:128, :, 1, :], C.KT[:, 64:128, 4096:4128].rearrange("c p n -> p c n"), BKPn)
        cr = [sb(f"cr{i}", [128, 512], F32) for i in range(2)]; Bcr = P.bufs_n("cr", 2)
        cb = [sb(f"cb{i}", [128, 256], BF16) for i in range(2)]; Bcb = P.bufs_n("cb", 2)
        KPs = [sb(f"KPs{i}", [128, 2, 2, 128], BF16) for i in range(2)]; BKPs = P.bufs_n("KPs", 2)
        VAs = [sb(f"VAs{i}", [128, 4, 66], BF16) for i in range(2)]; BVAs = P.bufs_n("VAs", 2)
        VAn = [sb(f"VAn{i}", [8, 264], BF16) for i in range(2)]; BVAn = P.bufs_n("VAn", 2)
        Qc = [sb(f"Qc{i}", [128, 2, 32], BF16) for i in range(2)]; BQc = P.bufs_n("Qc", 2)
        pTb = pTr
        for i in range(2):
            P.op("pool", lambda e, i=i: e.memset(KPs[i][:], 0.0), [], [BKPs[i]])
            P.op("pool", lambda e, i=i: e.memset(VAs[i][:], 1.0), [], [BVAs[i]])
        sample_jobs = []

        def sample_group(s_, g, rho, i2):
            Dg = DILS[g]
            n = (8 - rho + Dg - 1) // Dg
            W = WINS[g]
            rows = C.cache[g][s_, rho:W:Dg, :]
            P.load(cr[i2][:], rows, Bcr[i2])
            P.op("dve", lambda e: e.tensor_copy(out=cb[i2][:], in_=cr[i2][:, 0:256]), [Bcr[i2]], [Bcb[i2]])
            P.op("pool", lambda e: e.tensor_copy(out=VAs[i2][:, :, 0:64],
                                                 in_=cr[i2][:, 256:512].rearrange("p (h d) -> p h d", h=4)),
                 [Bcr[i2]], [BVAs[i2]])
            for m in range(2):
                P.tr(pTb[:, m * 128:(m + 1) * 128], cb[i2][:, m * 128:(m + 1) * 128], identb[:], [Bcb[i2], Bidb], [BpTr])
            P.op("act", lambda e: e.copy(out=KPs[i2][0:64, :, 0, :], in_=pTb[0:64, 0:256].rearrange("p (m n) -> p m n", m=2)),
                 [BpTr], [BKPs[i2]])
            P.op("dve", lambda e: e.tensor_copy(out=KPs[i2][64:128, :, 1, :],
                                                in_=pTb[64:128, 0:256].rearrange("p (m n) -> p m n", m=2)),
                 [BpTr], [BKPs[i2]])
            tk = 4096 + s_ * 8 + rho
            vsrc = C.VA[g, tk:tk + Dg * (n - 1) + 1:Dg, :]
            P.load(VAn[i2][0:n, :], vsrc, BVAn[i2])
            lc = slice(s_ * 8 + rho, s_ * 8 + rho + Dg * (n - 1) + 1, Dg)
            P.op("dve", lambda e: e.tensor_copy(out=Qc[i2][:, :, 0:4 * n].rearrange("p m (q n) -> p m q n", q=4),
                                                in_=Qn[:, g * 8:(g + 1) * 8, lc].rearrange("p (m q) n -> p m q n", m=2)),
                 [BQn], [BQc[i2]])
            q_of = lambda m: (Qc[i2][:, m, 0:4 * n], [BQc[i2]])
            kts = [
                (1, 128, (lambda m, half: KPs[i2][:, m, half, :]),
                 (lambda kvh: VAs[i2][:, kvh, 0:65]), [BKPs[i2]], [BVAs[i2]]),
                (0, n, (lambda m, half: KPn[:, g * 2 + m, half, lc]),
                 (lambda kvh: VAn[i2][0:n, kvh * 66:kvh * 66 + 65]), [BKPn], [BVAn[i2]]),
            ]
            orow = C.OG[g, tk:tk + Dg * (n - 1) + 1:Dg, :]
            attn_tile_small(g, q_of, kts, n, orow)

        ci = 0
        for s_ in range(4):
            for g in range(3):
                for rho in range(min(DILS[g], 8)):
                    sample_jobs.append((s_, g, rho, ci % 2))
                    ci += 1
        li = 0
        npg = [0]
        for sq in range(2):
            tok0 = sq * SEQ
            for g in range(3):
                Dg = DILS[g]
                ntc = 16 // Dg
                bi = li % 2
                li += 1
                Q, BQ = Qb[bi], BQb[bi]
                K, BK = KP[bi], BKP[bi]
                V, BV = Vb[bi], BVb[bi]
                P.load(Q[:], C.QT[g * 8:(g + 1) * 8, :, tok0:tok0 + SEQ].rearrange("c p n -> p c n"), BQ)
                for m in range(2):
                    P.load(K[0:64, m, 0, :], C.KT[g * 2 + m, 0:64, tok0:tok0 + SEQ], BK)
                    P.load(K[64:128, m, 1, :], C.KT[g * 2 + m, 64:128, tok0:tok0 + SEQ], BK)
                for rho in range(Dg):
                    src = C.VA[g, tok0 + rho:tok0 + SEQ:Dg, :].rearrange("(t c) n -> c t n", c=128)
                    P.load(V[:, rho * ntc:(rho + 1) * ntc, :], src, BV)
                for rho in range(Dg):
                    for tt in range(ntc):
                        def cols(t_):
                            a0 = rho + Dg * 128 * t_
                            return slice(a0, a0 + Dg * 127 + 1, Dg)
                        q_of = lambda m, Q=Q, BQ=BQ, c=cols(tt): (Q[:, m * 4:(m + 1) * 4, c], [BQ])
                        kts = []
                        kt_list = [(0, tt)] + ([(1, tt - 1)] if tt > 0 else [])
                        for kind, tk in kt_list:
                            c = cols(tk)
                            ct = rho * ntc + tk
                            kts.append((kind, 128,
                                        (lambda m, half, K=K, c=c: K[:, m, half, c]),
                                        (lambda kvh, V=V, ct=ct: V[:, ct, kvh * 66:kvh * 66 + 65]),
                                        [BK], [BV]))
                        a0 = tok0 + rho + Dg * 128 * tt
                        out_rows = C.OG[g, a0:a0 + Dg * 127 + 1:Dg, :]
                        attn_tile(g, q_of, kts, 128, out_rows)
                        npg[0] += 1
                        if npg[0] % 2 == 0 and sample_jobs:
                            sample_group(*sample_jobs.pop(0))

        while sample_jobs:
            sample_group(*sample_jobs.pop(0))
        attn_flush()
        P.emit()


def phase_oproj(C, wgu_next=None):
    nc = C.nc
    with ExitStack() as st:
        def sb(name, shape, dt):
            return st.enter_context(nc.sbuf_tensor("o_" + name, list(shape), dt))

        def ps(name, shape, dt):
            return st.enter_context(nc.psum_tensor("o_" + name, list(shape), dt))

        P = Phase(nc, "oproj")
        P.setup_cast(sb)
        wo = sb("wo", [128, 8, D], BF16); Bwo = P.buf("wo")
        g1 = sb("g1", [128, D], F32); Bg1 = P.buf("g1")
        cst = sb("cst", [128, 128], F32); Bcst = P.buf("cst")
        identb = sb("identb", [128, 128], BF16); Bidb = P.buf("identb")
        neghalf = sb("neghalf", [128, 8], F32); Bsm = P.buf("small")
        for kc in range(8):
            P.load_cast(lambda a, b, kc=kc: wo[:, kc, a:b], C.w_o_b[kc * 128:(kc + 1) * 128, :], D, Bwo)
        P.load(g1[:], C.gains[5], Bg1)
        P.load(cst[:], C.consts[0], Bcst)
        P.op("pool", lambda e: e.memset(neghalf[:], -0.5), [], [Bsm])
        P.op("dve", lambda e: e.tensor_copy(out=identb[:], in_=cst[:]), [Bcst], [Bidb])
        og3 = [sb(f"og3_{i}", [128, 3, OSLOT], F32) for i in range(3)]; Bog3 = P.bufs_n("og3_", 3)
        ht = [sb(f"ht{i}", [128, D], F32) for i in range(3)]; Bht = P.bufs_n("ht", 3)
        osum = sb("osum", [128, OSLOT], F32); Bosum = P.buf("osum")
        rden = sb("rden", [128, 16], F32); Brden = P.buf("rden")
        og = sb("og", [128, D], BF16); Bog = P.buf("og")
        ogT = [sb(f"ogT{i}", [128, 8, 128], BF16) for i in range(2)]; BogT = P.bufs_n("ogT", 2)
        junk = sb("junk", [128, D], BF16); Bjunk = P.buf("junk")
        ss3 = sb("ss3", [128, 8], F32); ms3 = sb("ms3", [128, 8], F32); rs3 = sb("rs3", [128, 8], F32)
        Bss3, Bms3, Brs3 = P.buf("ss3"), P.buf("ms3"), P.buf("rs3")
        ho = [sb(f"ho{i}", [128, D], F32) for i in range(2)]; Bho = P.bufs_n("ho", 2)
        pT = ps("pT", [128, 8, 128], BF16); BpT = P.buf("pT")
        pY = [ps(f"pY{i}", [128, 512], F32) for i in range(4)]; BpY = P.bufs_n("pY", 4)

        def load_t(t):
            P.load(og3[t % 3][:], C.OG[:, t * 128:(t + 1) * 128, :].rearrange("g p n -> p g n"), Bog3[t % 3])
            P.load(ht[t % 3][:], C.H2[t * 128:(t + 1) * 128, :], Bht[t % 3])

        def stage_a(t):
            O3 = og3[t % 3]; BO3 = Bog3[t % 3]
            P.op("dve", lambda e, O3=O3: e.tensor_tensor(out=osum[:], in0=O3[:, 0, :], in1=O3[:, 1, :], op=ALU.add), [BO3], [Bosum])
            P.op("dve", lambda e, O3=O3: e.tensor_tensor(out=osum[:], in0=osum[:], in1=O3[:, 2, :], op=ALU.add), [BO3, Bosum], [Bosum])
            P.op("dve", lambda e: e.reciprocal(out=rden[:], in_=osum[:].rearrange("p (h n) -> p h n", n=65)[:, :, 64]), [Bosum], [Brden])
            P.op("dve", lambda e: e.tensor_tensor(out=og[:].rearrange("p (h d) -> p h d", h=16),
                                                  in0=osum[:].rearrange("p (h n) -> p h n", n=65)[:, :, 0:64],
                                                  in1=rden[:].unsqueeze(2).broadcast_to([128, 16, 64]), op=ALU.mult),
                 [Bosum, Brden], [Bog])
            for kc in range(8):
                P.tr(pT[:, kc, :], og[:, kc * 128:(kc + 1) * 128], identb[:], [Bog, Bidb], [BpT])
            P.op("act", lambda e, t=t: e.copy(out=ogT[t % 2][:], in_=pT[:]), [BpT], [BogT[t % 2]])

        def stage_b(t):
            Ht = ht[t % 3]; BH = Bht[t % 3]
            yb = [pY[(2 * t) % 4], pY[(2 * t + 1) % 4]]
            Byb = [BpY[(2 * t) % 4], BpY[(2 * t + 1) % 4]]
            for c in range(2):
                for kc in range(8):
                    P.mm(yb[c][:], ogT[t % 2][:, kc, :], wo[:, kc, c * 512:(c + 1) * 512], kc == 0, kc == 7, [BogT[t % 2], Bwo], [Byb[c]])
            for c in range(2):
                P.op("act", lambda e, c=c, yb=yb: e.activation(out=junk[:, c * 512:(c + 1) * 512], in_=yb[c][:], func=AF.Square,
                                                             accum_out=ss3[:, c:c + 1]), [Byb[c]], [Bjunk, Bss3])
            P.op("dve", lambda e: e.tensor_tensor(out=ss3[:, 2:3], in0=ss3[:, 0:1], in1=ss3[:, 1:2], op=ALU.add), [Bss3], [Bss3])
            rstd_ops(P, Bsm, ss3[:, 2:3], 1, rs3[:, 0:1], ms3[:, 0:1], neghalf[:, 0:1], Bss3, Brs3, Bms3, D)
            Ho = ho[t % 2]; BHo = Bho[t % 2]
            for c in range(2):
                P.op("dve", lambda e, c=c, yb=yb, Ho=Ho: e.scalar_tensor_tensor(out=Ho[:, c * 512:(c + 1) * 512], in0=yb[c][:], scalar=rs3[:, 0:1],
                                                                              in1=g1[:, c * 512:(c + 1) * 512], op0=ALU.mult, op1=ALU.mult),
                     [Byb[c], Brs3, Bg1], [BHo])
            P.op("pool", lambda e, Ho=Ho, Ht=Ht: e.tensor_tensor(out=Ho[:], in0=Ho[:], in1=Ht[:], op=ALU.add), [BHo, BH], [BHo])
            P.store(C.H3[t * 128:(t + 1) * 128, :], Ho[:], BHo)

        pre = []
        inflight = []
        if wgu_next is not None:
            Bwn = P.buf("wgu_next")
            for kc in range(8):
                for c0 in range(0, 2 * DFF, 1024):
                    c1 = min(2 * DFF, c0 + 1024)
                    pre.append((kc, c0, c1))

        def prefetch(k):
            while inflight:
                i, kc, c0, c1 = inflight.pop(0)
                stg, Bs = P.stg[i], P.Bstg[i]
                dst = wgu_next[:, kc, c0:c1]
                n = c1 - c0
                if P.ncast % 2 == 0:
                    P.op("act", lambda e, stg=stg, dst=dst, n=n: e.copy(out=dst, in_=stg[:, 0:n]), [Bs], [Bwn])
                else:
                    P.op("dve", lambda e, stg=stg, dst=dst, n=n: e.tensor_copy(out=dst, in_=stg[:, 0:n]), [Bs], [Bwn])
                P.ncast += 1
            for j in range(k):
                if pre:
                    kc, c0, c1 = pre.pop(0)
                    i = (P.ncast + j) % 3
                    P.load(P.stg[i][:, 0:c1 - c0], C.w_gu[1, kc * 128:(kc + 1) * 128, c0:c1], P.Bstg[i])
                    inflight.append((i, kc, c0, c1))

        load_t(0)
        load_t(1)
        stage_a(0)
        for t in range(NT):
            if t + 2 < NT:
                load_t(t + 2)
            prefetch(2)
            if t + 1 < NT:
                stage_a(t + 1)
            stage_b(t)
        while pre or inflight:
            prefetch(2)
        P.emit()


def _t5_buckets(dist):
    d = np.asarray(dist)
    large = 16 + (np.log(np.maximum(d, 1) / 16) / np.log(2048 / 16) * (32 - 16)).astype(np.int64)
    large = np.minimum(large, 31)
    return np.where(d < 16, d, large).astype(np.int32)


def _bias_tables(rel_bias):
    out = np.full((3, 2, 2, 128, 2, 4, 128), -30000.0, np.float32)
    c = np.arange(128)[:, None]
    i = np.arange(128)[None, :]
    for g in range(3):
        bk = _t5_buckets(DILS[g] * np.arange(129))
        for m in range(2):
            for half in range(2):
                for qpk in range(4):
                    hh = (2 * m + half) * 4 + qpk
                    tab = rel_bias[bk, g * 16 + hh]
                    j0 = i - c
                    out[g, m, half, :, 0, qpk, :] = np.where(j0 >= 0, tab[np.clip(j0, 0, 128)], -30000.0)
                    j1 = i - c + 128
                    out[g, m, half, :, 1, qpk, :] = np.where(j1 <= 128, tab[np.clip(j1, 0, 128)], -30000.0)
    return out.reshape(12, 128, 1024)


def _consts():
    cs = np.zeros((8, 128, 128), np.float32)
    j = np.arange(128)[:, None]
    i = np.arange(128)[None, :]
    cs[0] = np.eye(128, dtype=np.float32)
    cs[2] = (j <= i).astype(np.float32)
    cs[1] = cs[2] * (-1.0 / 16.0)
    same = (j // 8 == i // 8) & (j < 32) & (i < 32)
    cs[4] = ((j <= i) & same).astype(np.float32)
    cs[3] = cs[4] * (-1.0 / 16.0)
    cs[5][:, 0] = -1.0 / 16.0
    for s in range(4):
        cs[5][8 * s:8 * s + 8, 1 + s] = -1.0 / 16.0
        cs[6][8 * s:8 * s + 8, s] = 1.0
    return cs


_NC_CACHE = {}


def kernel(x_prompt, x_sample, state_gla, cache_win1, cache_win2, cache_win3, norm_g, w_in_a,
           w_a2, b_a, g_onorm, w_o_a, g_kv, w_kv, w_q_b, w_o_b, rel_bias, w_gate_up, w_down):
    f = lambda a: np.ascontiguousarray(np.asarray(a, dtype=np.float32))
    x_prompt, x_sample, state_gla = f(x_prompt), f(x_sample), f(state_gla)
    caches = [f(cache_win1), f(cache_win2), f(cache_win3)]
    norm_g, rel_bias = f(norm_g), f(rel_bias)
    if "nc" not in _NC_CACHE:
        _NC_CACHE["nc"] = build()
    nc = _NC_CACHE["nc"]
    gains = np.concatenate([norm_g.reshape(8, 1, D), f(g_kv).reshape(1, 1, D)], axis=0)
    gains = np.ascontiguousarray(np.broadcast_to(gains, (9, 128, D)))
    gon = np.ascontiguousarray(np.broadcast_to(f(g_onorm).reshape(1, 256), (128, 256)))
    wq = f(w_q_b)[0].reshape(D, 3, 2, 2, 4, 64).transpose(0, 1, 2, 4, 3, 5).reshape(D, 3072)
    shared = {
        "gains": gains, "gon": gon, "w_in": f(w_in_a)[0], "w_a2": f(w_a2)[0], "b_a": f(b_a).reshape(1, 512),
        "w_o_a": f(w_o_a)[0], "w_kv": f(w_kv), "w_q": np.ascontiguousarray(wq), "w_o_b": f(w_o_b)[0],
        "w_gu": f(w_gate_up), "w_dn": f(w_down), "btab": _bias_tables(rel_bias), "consts": _consts(),
    }
    in_maps = []
    for c in range(NCORES):
        xs = np.zeros((NTOK, D), np.float32)
        xs[0:4096] = x_prompt[2 * c:2 * c + 2].reshape(4096, D)
        xs[4096:4128] = x_sample[4 * c:4 * c + 4].reshape(32, D)
        m = dict(shared)
        m["x"] = xs
        m["state"] = np.ascontiguousarray(state_gla[0, 4 * c:4 * c + 4])
        for g in range(3):
            m[f"cache{g}"] = np.ascontiguousarray(caches[g][4 * c:4 * c + 4].reshape(4, WINS[g], 512))
        in_maps.append(m)
    res = run_bass_kernel_spmd(nc, in_maps, core_ids=list(range(NCORES)))
    R = res.results
    y_prompt = np.concatenate([R[c]["y"][0:4096].reshape(2, SEQ, D) for c in range(NCORES)], axis=0)
    y_sample = np.concatenate([R[c]["y"][4096:4128].reshape(4, 8, D) for c in range(NCORES)], axis=0)
    gla_p = np.concatenate([R[c]["gla_p"] for c in range(NCORES)], axis=0)[None]
    gla_s = np.concatenate([R[c]["gla_s"] for c in range(NCORES)], axis=0)[None]
    wp = [np.concatenate([R[c][f"win{g}_p"] for c in range(NCORES)], axis=0).reshape(16, WINS[g], 2, 4, 64) for g in range(3)]
    ws = [np.concatenate([R[c][f"win{g}_s"] for c in range(NCORES)], axis=0).reshape(32, WINS[g], 2, 4, 64) for g in range(3)]
    if DEBUG:
        kernel.debug = R
    return (y_prompt, y_sample, gla_p, wp[0], wp[1], wp[2], gla_s, ws[0], ws[1], ws[2])
```
